# Optimizing a Trainium2 kernel written in Bass

```python
import jax, jax.numpy as jnp
from jax import lax
import numpy as np

D_MODEL = 2048
BATCH = 4
SEQ = 2048
DEPTH = 1
DEC_BATCH = 128
DEC_SEQ = 1
PAST_LEN = 2048
PAGE_SIZE = 128

N_HEADS = 8
HEAD_DIM = 128
KV_HEADS = 2
HPG = N_HEADS // KV_HEADS
ATTN_DIM = N_HEADS * HEAD_DIM
KV_DIM = KV_HEADS * HEAD_DIM
POOL_DIM = D_MODEL - ATTN_DIM
POOL_WINDOWS = (2, 4, 8, 16)
POOL_GROUPS = len(POOL_WINDOWS)
POOL_GW = POOL_DIM // POOL_GROUPS
POOL_BUF = max(POOL_WINDOWS) - 1
N_BRANCH = 3
PROJ_DIM = ATTN_DIM + 2 * N_BRANCH * KV_DIM + N_BRANCH * N_HEADS + POOL_DIM
ROT_DIM = HEAD_DIM // 4
ROPE_THETA = 500000.0
CMP_LEN = 32
CMP_STRIDE = 16
CMP_HID = HEAD_DIM
SEL_BLOCK = 64
SEL_TOPK = 16
WINDOW = 512
WIN_QBLK = 128
SEL_QBLK = 64
D_FF = 4 * D_MODEL
EPS = 1e-6
SCALE = HEAD_DIM ** -0.5
FORCE_SCORE = 1e4
NEG_INF = -1e30

kernel_name = 'nsa_pool_hybrid_step'


def rms_norm(x, g):
    x32 = x.astype(jnp.float32)
    y = x32 * lax.rsqrt(jnp.mean(x32 * x32, axis=-1, keepdims=True) + EPS)
    return (y * g.astype(jnp.float32)).astype(x.dtype)


def masked_softmax(s, mask):
    s = jnp.where(mask, s.astype(jnp.float32), NEG_INF)
    m = jnp.max(s, axis=-1, keepdims=True)
    e = jnp.where(mask, jnp.exp(s - m), 0.0)
    d = jnp.sum(e, axis=-1, keepdims=True)
    return e / jnp.where(d > 0, d, 1.0)


def partial_rope(x, pos):
    half = ROT_DIM // 2
    inv = jnp.power(ROPE_THETA, -jnp.arange(half, dtype=jnp.float32) * (2.0 / ROT_DIM))
    ang = pos.astype(jnp.float32)[:, None] * inv[None, :]
    cos = jnp.cos(ang)[None, :, None, :]
    sin = jnp.sin(ang)[None, :, None, :]
    x32 = x.astype(jnp.float32)
    x1, x2 = x32[..., :half], x32[..., half:ROT_DIM]
    out = jnp.concatenate([x1 * cos - x2 * sin, x1 * sin + x2 * cos, x32[..., ROT_DIM:]], axis=-1)
    return out.astype(x.dtype)


def project(h, w_in, pos):
    B, T, _ = h.shape
    z = h @ w_in
    sizes = [ATTN_DIM] + [KV_DIM] * (2 * N_BRANCH) + [N_BRANCH * N_HEADS, POOL_DIM]
    offs = np.cumsum(sizes)[:-1].tolist()
    q, kc, vc, ks, vs, kw, vw, gl, u = jnp.split(z, offs, axis=-1)
    kvh = lambda a: a.reshape(B, T, KV_HEADS, HEAD_DIM)
    q = partial_rope(q.reshape(B, T, N_HEADS, HEAD_DIM), pos)
    kc, ks, kw = partial_rope(kvh(kc), pos), partial_rope(kvh(ks), pos), partial_rope(kvh(kw), pos)
    vc, vs, vw = kvh(vc), kvh(vs), kvh(vw)
    gates = jax.nn.sigmoid(gl.astype(jnp.float32)).reshape(B, T, N_HEADS, N_BRANCH)
    return q, kc, vc, ks, vs, kw, vw, gates, u


def compress(rows, pos_emb, w1, w2):
    B, L, G, D = rows.shape
    r = CMP_LEN // CMP_STRIDE
    n_chunk = L // CMP_STRIDE
    ch = rows[:, :n_chunk * CMP_STRIDE].reshape(B, n_chunk, CMP_STRIDE, G, D)
    n_cmp = n_chunk - r + 1
    blocks = jnp.concatenate([ch[:, i:i + n_cmp] for i in range(r)], axis=2)
    blocks = blocks + pos_emb[None, None, :, None, :]
    flat = blocks.transpose(0, 1, 3, 2, 4).reshape(B, n_cmp, G, CMP_LEN * D)
    return jax.nn.gelu(flat @ w1) @ w2


def cmp_branch(q, k_rows, v_rows, q_pos, cmp_w):
    kc = compress(k_rows, *cmp_w[0])
    vc = compress(v_rows, *cmp_w[1])
    B, T = q.shape[:2]
    n_cmp = kc.shape[1]
    blk_end = jnp.arange(n_cmp) * CMP_STRIDE + CMP_LEN - 1
    mask = blk_end[None, :] <= q_pos[:, None]
    qg = q.reshape(B, T, KV_HEADS, HPG, HEAD_DIM)
    s = jnp.einsum('btghd,bngd->bghtn', qg, kc).astype(jnp.float32) * SCALE
    p = masked_softmax(s, mask)
    o = jnp.einsum('bghtn,bngd->btghd', p, vc.astype(jnp.float32))
    return o.reshape(B, T, N_HEADS, HEAD_DIM).astype(q.dtype), p


def select_blocks(p_cmp, q_pos, n_sel):
    n_cmp = p_cmp.shape[-1]
    cs = jnp.arange(n_cmp) * CMP_STRIDE
    ss = jnp.arange(n_sel) * SEL_BLOCK
    cmp_to_sel = ((cs[:, None] < ss[None, :] + SEL_BLOCK) & (cs[:, None] + CMP_LEN > ss[None, :])).astype(jnp.float32)
    imp = jnp.einsum('bghtn,ns->bgts', p_cmp, cmp_to_sel)
    j = jnp.arange(n_sel)[None, :]
    jt = (q_pos // SEL_BLOCK)[:, None]
    force = (j == 0) | (j == jt) | (j == jt - 1)
    valid = j <= jt
    score = jnp.where(force, FORCE_SCORE, jnp.where(valid, imp, -1.0))
    _, idx = lax.top_k(score, min(SEL_TOPK, n_sel))
    return idx


def to_blocks(rows, n_sel):
    B, L, G, D = rows.shape
    rows = jnp.pad(rows, ((0, 0), (0, n_sel * SEL_BLOCK - L), (0, 0), (0, 0)))
    return rows.reshape(B, n_sel, SEL_BLOCK, G, D).transpose(0, 3, 1, 2, 4)


def sel_attend(q, kb, vb, idx, q_pos):
    B, Tq = q.shape[:2]
    take = jax.vmap(jax.vmap(lambda blocks, ix: blocks[ix]))
    kg = take(kb, idx).reshape(B, KV_HEADS, Tq, -1, HEAD_DIM)
    vg = take(vb, idx).reshape(B, KV_HEADS, Tq, -1, HEAD_DIM)
    kpos = (idx[..., None] * SEL_BLOCK + jnp.arange(SEL_BLOCK)).reshape(B, KV_HEADS, Tq, -1)
    mask = (kpos <= q_pos[None, None, :, None])[:, :, None]
    qg = q.reshape(B, Tq, KV_HEADS, HPG, HEAD_DIM)
    s = jnp.einsum('btghd,bgtkd->bghtk', qg, kg).astype(jnp.float32) * SCALE
    p = masked_softmax(s, mask)
    o = jnp.einsum('bghtk,bgtkd->btghd', p, vg.astype(jnp.float32))
    return o.reshape(B, Tq, N_HEADS, HEAD_DIM).astype(q.dtype)


def sel_attend_blocked(q, kb, vb, idx, q_pos):
    B, T = q.shape[:2]
    nc = T // SEL_QBLK
    qs = q.reshape(B, nc, SEL_QBLK, N_HEADS, HEAD_DIM).swapaxes(0, 1)
    ids = idx.reshape(B, KV_HEADS, nc, SEL_QBLK, -1).transpose(2, 0, 1, 3, 4)
    ps = q_pos.reshape(nc, SEL_QBLK)
    out = lax.map(lambda a: sel_attend(a[0], kb, vb, a[1], a[2]), (qs, ids, ps))
    return out.swapaxes(0, 1).reshape(B, T, N_HEADS, HEAD_DIM)


def win_attend_band(q, k, v):
    B, T = q.shape[:2]
    nb = T // WIN_QBLK
    nprev = WINDOW // WIN_QBLK
    pad = ((0, 0), (WINDOW, 0), (0, 0), (0, 0))
    kp = jnp.pad(k, pad).reshape(B, nb + nprev, WIN_QBLK, KV_HEADS, HEAD_DIM)
    vp = jnp.pad(v, pad).reshape(B, nb + nprev, WIN_QBLK, KV_HEADS, HEAD_DIM)
    kband = jnp.concatenate([kp[:, i:i + nb] for i in range(nprev + 1)], axis=2)
    vband = jnp.concatenate([vp[:, i:i + nb] for i in range(nprev + 1)], axis=2)
    q_pos = jnp.arange(T).reshape(nb, WIN_QBLK)
    k_pos = (jnp.arange(nb) * WIN_QBLK)[:, None] - WINDOW + jnp.arange((nprev + 1) * WIN_QBLK)[None, :]
    kq = k_pos[:, None, :]
    qq = q_pos[:, :, None]
    mask = (kq <= qq) & (kq > qq - WINDOW) & (kq >= 0)
    qb = q.reshape(B, nb, WIN_QBLK, KV_HEADS, HPG, HEAD_DIM)
    s = jnp.einsum('bnqghd,bnkgd->bghnqk', qb, kband).astype(jnp.float32) * SCALE
    p = masked_softmax(s, mask)
    o = jnp.einsum('bghnqk,bnkgd->bnqghd', p, vband.astype(jnp.float32))
    return o.reshape(B, T, N_HEADS, HEAD_DIM).astype(q.dtype)


def win_attend_dense(q, k, v, q_pos, k_pos):
    B, T = q.shape[:2]
    kq = k_pos[None, :]
    qq = q_pos[:, None]
    mask = (kq <= qq) & (kq > qq - WINDOW) & (kq >= 0)
    qg = q.reshape(B, T, KV_HEADS, HPG, HEAD_DIM)
    s = jnp.einsum('btghd,bkgd->bghtk', qg, k).astype(jnp.float32) * SCALE
    p = masked_softmax(s, mask)
    o = jnp.einsum('bghtk,bkgd->btghd', p, v.astype(jnp.float32))
    return o.reshape(B, T, N_HEADS, HEAD_DIM).astype(q.dtype)


def pool_mix(u_ext, t_pos, pool_w, pool_scale):
    B, Lx, C = u_ext.shape
    T = Lx - POOL_BUF
    c = jnp.cumsum(u_ext.astype(jnp.float32), axis=1)
    c = jnp.concatenate([jnp.zeros((B, 1, C), jnp.float32), c], axis=1)
    end = c[:, POOL_BUF + 1:]
    u_new = u_ext[:, POOL_BUF:].astype(jnp.float32)
    diffs = []
    for g, w in enumerate(POOL_WINDOWS):
        lo, hi = g * POOL_GW, (g + 1) * POOL_GW
        start = c[:, POOL_BUF + 1 - w:POOL_BUF + 1 - w + T, lo:hi]
        cnt = jnp.minimum(w, t_pos + 1).astype(jnp.float32)[None, :, None]
        diffs.append((end[..., lo:hi] - start) / cnt - u_new[..., lo:hi])
    d = jnp.stack(diffs, axis=2)
    y = jnp.einsum('btgc,gce->btge', d, pool_w.astype(jnp.float32)).reshape(B, T, C)
    return (y * pool_scale.astype(jnp.float32)).astype(u_ext.dtype)


def merge(o_cmp, o_sel, o_win, gates, y_pool, w_o):
    B, T = o_cmp.shape[:2]
    o = gates[..., 0:1] * o_cmp + gates[..., 1:2] * o_sel + gates[..., 2:3] * o_win
    o = o.astype(y_pool.dtype).reshape(B, T, ATTN_DIM)
    return jnp.concatenate([o, y_pool], axis=-1) @ w_o


def mixer_prompt(h, w_in, cmp_w, pool_w, pool_scale, w_o):
    B, T, _ = h.shape
    pos = jnp.arange(T, dtype=jnp.int32)
    q, kc, vc, ks, vs, kw, vw, gates, u = project(h, w_in, pos)
    o_cmp, p_cmp = cmp_branch(q, kc, vc, pos, cmp_w)
    n_sel = -(-T // SEL_BLOCK)
    idx = select_blocks(p_cmp, pos, n_sel)
    o_sel = sel_attend_blocked(q, to_blocks(ks, n_sel), to_blocks(vs, n_sel), idx, pos)
    o_win = win_attend_band(q, kw, vw)
    u_ext = jnp.concatenate([jnp.zeros((B, POOL_BUF, POOL_DIM), u.dtype), u], axis=1)
    y_pool = pool_mix(u_ext, pos, pool_w, pool_scale)
    out = merge(o_cmp, o_sel, o_win, gates, y_pool, w_o)
    wb = min(WINDOW, T)
    kv_rows = jnp.stack([kc, vc, ks, vs], axis=2)
    win_state = jnp.stack([kw[:, T - wb:], vw[:, T - wb:]], axis=2)
    return out, kv_rows, win_state, u_ext[:, -POOL_BUF:]


def mixer_sample(h, cache_kv, win_kv, pool_state, page_table, w_in, cmp_w, pool_w, pool_scale, w_o):
    B, T, _ = h.shape
    n_pages = page_table.shape[1]
    past_len = n_pages * cache_kv.shape[1]
    pos = past_len + jnp.arange(T, dtype=jnp.int32)
    q, kc, vc, ks, vs, kw, vw, gates, u = project(h, w_in, pos)
    past = cache_kv[page_table].reshape(B, past_len, 4, KV_HEADS, HEAD_DIM)
    kc_all = jnp.concatenate([past[:, :, 0], kc], axis=1)
    vc_all = jnp.concatenate([past[:, :, 1], vc], axis=1)
    ks_all = jnp.concatenate([past[:, :, 2], ks], axis=1)
    vs_all = jnp.concatenate([past[:, :, 3], vs], axis=1)
    L = past_len + T
    o_cmp, p_cmp = cmp_branch(q, kc_all, vc_all, pos, cmp_w)
    n_sel = -(-L // SEL_BLOCK)
    idx = select_blocks(p_cmp, pos, n_sel)
    o_sel = sel_attend(q, to_blocks(ks_all, n_sel), to_blocks(vs_all, n_sel), idx, pos)
    wb = win_kv.shape[1]
    kw_all = jnp.concatenate([win_kv[:, :, 0], kw], axis=1)
    vw_all = jnp.concatenate([win_kv[:, :, 1], vw], axis=1)
    k_pos = past_len - wb + jnp.arange(wb + T, dtype=jnp.int32)
    o_win = win_attend_dense(q, kw_all, vw_all, pos, k_pos)
    u_ext = jnp.concatenate([pool_state, u], axis=1)
    y_pool = pool_mix(u_ext, pos, pool_w, pool_scale)
    out = merge(o_cmp, o_sel, o_win, gates, y_pool, w_o)
    kv_rows = jnp.stack([kc, vc, ks, vs], axis=2)
    win_state = jnp.stack([kw_all[:, -wb:], vw_all[:, -wb:]], axis=2)
    return out, kv_rows, win_state, u_ext[:, -POOL_BUF:]


def sq_relu_mlp(h, w_up, w_down):
    a = jax.nn.relu(h @ w_up)
    return (a * a) @ w_down


def setup_inputs(seed: int = 0) -> dict:
    key = jax.random.key(seed)
    ks = jax.random.split(key, 24)
    f32 = jnp.float32
    nrm = lambda k, shape, scale: jax.random.normal(k, shape, f32) * scale
    gain = lambda k: 1.0 + 0.05 * jax.random.normal(k, (DEPTH, D_MODEL), f32)
    n_pages = PAST_LEN // PAGE_SIZE
    n_used = DEC_BATCH * n_pages
    n_phys = n_used + max(1, n_used // 4)
    wb = min(WINDOW, PAST_LEN)
    page_table = jax.random.permutation(ks[5], n_phys)[:n_used].reshape(DEC_BATCH, n_pages).astype(jnp.int32)
    return {
        'x_prompt': nrm(ks[0], (BATCH, SEQ, D_MODEL), 1.0),
        'x_sample': nrm(ks[1], (DEC_BATCH, DEC_SEQ, D_MODEL), 1.0),
        'cache_kv': nrm(ks[2], (DEPTH, n_phys, PAGE_SIZE, 4, KV_HEADS, HEAD_DIM), 1.0),
        'state_win_kv': nrm(ks[3], (DEPTH, DEC_BATCH, wb, 2, KV_HEADS, HEAD_DIM), 1.0),
        'state_pool': nrm(ks[4], (DEPTH, DEC_BATCH, POOL_BUF, POOL_DIM), 1.0),
        'page_table': page_table,
        'norm_mix_pre': gain(ks[6]),
        'w_in': nrm(ks[7], (DEPTH, D_MODEL, PROJ_DIM), D_MODEL ** -0.5),
        'cmp_pos_k': nrm(ks[8], (DEPTH, CMP_LEN, HEAD_DIM), 0.1),
        'cmp_w1_k': nrm(ks[9], (DEPTH, CMP_LEN * HEAD_DIM, CMP_HID), (CMP_LEN * HEAD_DIM) ** -0.5),
        'cmp_w2_k': nrm(ks[10], (DEPTH, CMP_HID, HEAD_DIM), CMP_HID ** -0.5),
        'cmp_pos_v': nrm(ks[11], (DEPTH, CMP_LEN, HEAD_DIM), 0.1),
        'cmp_w1_v': nrm(ks[12], (DEPTH, CMP_LEN * HEAD_DIM, CMP_HID), (CMP_LEN * HEAD_DIM) ** -0.5),
        'cmp_w2_v': nrm(ks[13], (DEPTH, CMP_HID, HEAD_DIM), CMP_HID ** -0.5),
        'pool_w': nrm(ks[14], (DEPTH, POOL_GROUPS, POOL_GW, POOL_GW), POOL_GW ** -0.5),
        'pool_scale': 0.5 + 0.1 * jax.random.normal(ks[15], (DEPTH, POOL_DIM), f32),
        'w_o': nrm(ks[16], (DEPTH, D_MODEL, D_MODEL), D_MODEL ** -0.5),
        'norm_mix_post': gain(ks[17]),
        'norm_mlp_pre': gain(ks[18]),
        'w_up': nrm(ks[19], (DEPTH, D_MODEL, D_FF), D_MODEL ** -0.5),
        'w_down': nrm(ks[20], (DEPTH, D_FF, D_MODEL), D_FF ** -0.5),
        'norm_mlp_post': gain(ks[21]),
    }


def reference(x_prompt, x_sample, cache_kv, state_win_kv, state_pool, page_table,
              norm_mix_pre, w_in, cmp_pos_k, cmp_w1_k, cmp_w2_k, cmp_pos_v, cmp_w1_v, cmp_w2_v,
              pool_w, pool_scale, w_o, norm_mix_post, norm_mlp_pre, w_up, w_down, norm_mlp_post):
    y_p, y_s = x_prompt, x_sample
    kv_p, kv_s, win_p, win_s, pool_p, pool_s = [], [], [], [], [], []
    for l in range(DEPTH):
        cmp_w = ((cmp_pos_k[l], cmp_w1_k[l], cmp_w2_k[l]), (cmp_pos_v[l], cmp_w1_v[l], cmp_w2_v[l]))
        mw = (w_in[l], cmp_w, pool_w[l], pool_scale[l], w_o[l])
        m, kv_r, w_st, p_st = mixer_prompt(rms_norm(y_p, norm_mix_pre[l]), *mw)
        y_p = y_p + rms_norm(m, norm_mix_post[l])
        y_p = y_p + rms_norm(sq_relu_mlp(rms_norm(y_p, norm_mlp_pre[l]), w_up[l], w_down[l]), norm_mlp_post[l])
        kv_p.append(kv_r)
        win_p.append(w_st)
        pool_p.append(p_st)
        m, kv_r, w_st, p_st = mixer_sample(rms_norm(y_s, norm_mix_pre[l]), cache_kv[l], state_win_kv[l],
                                           state_pool[l], page_table, *mw)
        y_s = y_s + rms_norm(m, norm_mix_post[l])
        y_s = y_s + rms_norm(sq_relu_mlp(rms_norm(y_s, norm_mlp_pre[l]), w_up[l], w_down[l]), norm_mlp_post[l])
        kv_s.append(kv_r)
        win_s.append(w_st)
        pool_s.append(p_st)
    kv_prompt = jnp.stack(kv_p)
    kv_sample = jnp.stack(kv_s)
    win_prompt = jnp.stack(win_p)
    win_sample = jnp.stack(win_s)
    pool_prompt = jnp.stack(pool_p)
    pool_sample = jnp.stack(pool_s)
    return (y_p, y_s, kv_prompt, kv_sample, win_prompt, win_sample, pool_prompt, pool_sample)
```

```python
from contextlib import ExitStack
import numpy as np
import concourse.bass as bass
import concourse.mybir as mybir
from concourse.bass_utils import run_bass_kernel_spmd

dt = mybir.dt
F32 = dt.float32
BF16 = dt.bfloat16
I32 = dt.int32
AF = mybir.ActivationFunctionType
ALU = mybir.AluOpType
AX = mybir.AxisListType

ENGS = ("pe", "act", "dve", "pool", "sp")

D_MODEL = 2048
D_FF = 8192
PROJ = 3608
NT = 8
NS = 16
NTT = NT + 1
TOK = NTT * 128
ROT = 32
HALF = 16
THETA = 500000.0
EPS = 1e-6
NEG = -30000.0
SCALE = 128 ** -0.5
import os
import time
DEBUG = bool(os.environ.get("KDEBUG"))
KSKIP = bool(os.environ.get("KSKIP"))
KSTOP = int(os.environ.get("KSTOP", "99"))


class Buf:
    __slots__ = ("name", "w", "r", "sem", "dcnt")

    def __init__(self, name):
        self.name = name
        self.w = None
        self.r = []
        self.sem = None
        self.dcnt = 0


class Prog:
    def __init__(self, nc, es):
        self.nc = nc
        self.es = es
        self.ges = es
        self.items = {e: [] for e in ENGS}
        self.cnt = {e: 0 for e in ENGS}
        self.sem = {e: es.enter_context(nc.semaphore("sem_" + e)) for e in ENGS}
        self.seen = {e: {} for e in ENGS}
        self.dmabufs = []
        self.nbuf = 0
        self.mute = False

    def sb(self, name, shape, dtype):
        return self.es.enter_context(self.nc.sbuf_tensor(name, list(shape), dtype))

    def ps(self, name, shape, dtype):
        return self.es.enter_context(self.nc.psum_tensor(name, list(shape), dtype))

    def buf(self, name=None):
        self.nbuf += 1
        return Buf((name or "b") + f"_{self.nbuf}")

    def bufs(self, name, n):
        return [self.buf(f"{name}{i}") for i in range(n)]

    def _bufsem(self, b):
        if b.sem is None:
            b.sem = self.ges.enter_context(self.nc.semaphore("sd_" + b.name))
            self.dmabufs.append(b)
        return b.sem

    def _need(self, eng, dep, waits):
        if dep is None:
            return
        if dep[0] == "eng":
            _, e, idx = dep
            if e == eng and e == "pe":
                return
            key = ("eng", e)
            semh = self.sem[e]
            val = idx
        else:
            _, b, cnt = dep
            key = ("dma", id(b))
            semh = self._bufsem(b)
            val = cnt
        if self.seen[eng].get(key, 0) >= val:
            return
        prev = waits.get(key)
        if prev is None or prev[1] < val:
            waits[key] = (semh, val)

    def _deps(self, eng, reads, writes, same_eng_war=False):
        waits = {}
        for b in reads:
            self._need(eng, b.w, waits)
        for b in writes:
            self._need(eng, b.w, waits)
            for r in b.r:
                if r[0] == "eng" and r[1] == eng and not same_eng_war:
                    continue
                self._need(eng, r, waits)
        for key, (semh, val) in waits.items():
            self.items[eng].append(("wait", semh, val))
            self.seen[eng][key] = val

    def op(self, eng, fn, reads=(), writes=()):
        if self.mute:
            return 0
        self._deps(eng, reads, writes)
        self.cnt[eng] += 1
        idx = self.cnt[eng]
        self.items[eng].append(("op", fn, self.sem[eng], 1))
        tag = ("eng", eng, idx)
        for b in reads:
            b.r.append(tag)
        for b in writes:
            b.w = tag
            b.r = []
        return idx

    def dma(self, q, out, in_, reads=(), writes=(), track=None, **kw):
        if self.mute:
            return None
        self._deps(q, reads, writes, same_eng_war=True)
        b = track or (writes[0] if writes else reads[0])
        semh = self._bufsem(b)
        b.dcnt += 16
        self.items[q].append(("dma", lambda e: e.dma_start(out=out, in_=in_, **kw), semh, 16))
        tag = ("dma", b, b.dcnt)
        for rb in reads:
            rb.r.append(tag)
        for wb in writes:
            wb.w = tag
            wb.r = []
        return tag

    def idma(self, out, in_, idx_ap, nrows, reads=(), writes=()):
        if self.mute:
            return None
        q = "pool"
        self._deps(q, reads, writes, same_eng_war=True)
        b = writes[0]
        semh = self._bufsem(b)
        b.dcnt += 16
        self.items[q].append(("dma", lambda e: e.indirect_dma_start(
            out=out, out_offset=None, in_=in_, in_offset=bass.IndirectOffsetOnAxis(ap=idx_ap, axis=0)), semh, 16))
        tag = ("dma", b, b.dcnt)
        for rb in reads:
            rb.r.append(tag)
        for wb in writes:
            wb.w = tag
            wb.r = []
        return tag

    def barrier(self):
        for e in ENGS:
            for f in ENGS:
                if f != e and f != "sp" and self.cnt[f] > self.seen[e].get(("eng", f), 0):
                    self.items[e].append(("wait", self.sem[f], self.cnt[f]))
                    self.seen[e][("eng", f)] = self.cnt[f]
            for b in self.dmabufs:
                key = ("dma", id(b))
                if b.dcnt > self.seen[e].get(key, 0):
                    self.items[e].append(("wait", b.sem, b.dcnt))
                    self.seen[e][key] = b.dcnt

    def emit(self):
        nc = self.nc
        items = self.items
        self.items = {e: [] for e in ENGS}
        with nc.Block() as block:
            def run(engname):
                def body(e):
                    for it in items[engname]:
                        if it[0] == "wait":
                            e.wait_ge(it[1], it[2])
                        else:
                            it[1](e).then_inc(it[2], it[3])
                return body
            block.tensor(run("pe"))
            block.scalar(run("act"))
            block.vector(run("dve"))
            block.gpsimd(run("pool"))
            block.sync(run("sp"))

    def mm(self, out, lhsT, rhs, start, stop, reads, writes):
        return self.op("pe", lambda e: e.matmul(out, lhsT=lhsT, rhs=rhs, start=start, stop=stop, skip_group_check=True), reads, writes)

    def tr(self, out, in_, ident, reads, writes):
        return self.op("pe", lambda e: e.transpose(out=out, in_=in_, identity=ident), reads, writes)

    def act(self, out, in_, func, reads, writes, **kw):
        return self.op("act", lambda e: e.activation(out=out, in_=in_, func=func, **kw), reads, writes)

    def copy(self, eng, out, in_, reads, writes):
        if eng == "act":
            return self.op("act", lambda e: e.copy(out=out, in_=in_), reads, writes)
        return self.op(eng, lambda e: e.tensor_copy(out=out, in_=in_), reads, writes)

    def tt(self, eng, out, in0, in1, op, reads, writes):
        return self.op(eng, lambda e: e.tensor_tensor(out=out, in0=in0, in1=in1, op=op), reads, writes)

    def ts(self, eng, out, in0, s1, s2, op0, op1, reads, writes):
        if op1 is None:
            return self.op(eng, lambda e: e.tensor_scalar(out=out, in0=in0, scalar1=s1, scalar2=None, op0=op0), reads, writes)
        return self.op(eng, lambda e: e.tensor_scalar(out=out, in0=in0, scalar1=s1, scalar2=s2, op0=op0, op1=op1), reads, writes)

    def stt(self, eng, out, in0, scalar, in1, op0, op1, reads, writes):
        return self.op(eng, lambda e: e.scalar_tensor_tensor(out=out, in0=in0, scalar=scalar, in1=in1, op0=op0, op1=op1), reads, writes)

    def memset(self, eng, ap, val, writes):
        return self.op(eng, lambda e: e.memset(ap, val), (), writes)


def build_nc(nrows):
    nc = bass.Bass("TRN2", target_bir_lowering=False)
    din = lambda name, shape, d=F32: nc.dram_tensor(name, list(shape), d, kind="ExternalInput").ap()
    dout = lambda name, shape, d=F32: nc.dram_tensor(name, list(shape), d, kind="ExternalOutput").ap()
    xo = din("xo", [NT * 128, D_MODEL])
    xc = din("xc", [NT * 128, D_MODEL])
    xs = din("xs", [NS, D_MODEL])
    g_pre = din("g_pre", [1, D_MODEL])
    g_post = din("g_post", [1, D_MODEL])
    g_mpre = din("g_mpre", [1, D_MODEL])
    g_mpost = din("g_mpost", [1, D_MODEL])
    w_in = din("w_in", [D_MODEL, PROJ])
    w_o = din("w_o", [D_MODEL, D_MODEL])
    w_up = din("w_up", [D_MODEL, D_FF])
    w_down = din("w_down", [D_FF, D_MODEL])
    cs_o = din("cs_o", [NT * 128, 32])
    cs_c = din("cs_c", [NT * 128, 32])
    cs_s = din("cs_s", [128, 32])
    st_win = din("st_win", [NS, 512, 512])
    st_pool = din("st_pool", [NS * 15, 1024])
    pool_w = din("pool_w", [4, 256, 256])
    pool_sc = din("pool_sc", [1024])
    pinv = din("pinv", [128, 64])
    cw1 = [din("cw1k", [4096, 128]), din("cw1v", [4096, 128])]
    cw2 = [din("cw2k", [128, 128]), din("cw2v", [128, 128])]
    cpos = [din("cposk", [32, 128]), din("cposv", [32, 128])]
    t_tri4 = din("t_tri4", [128, 4 * 512])
    t_wband = din("t_wband", [128, 8 * 512])
    t_cmpb = din("t_cmpb", [128, 1024])
    t_tk = din("t_tk", [128, 3 * 8 * 32])
    t_esel = din("t_esel", [32, 16 * 128])
    t_csel = din("t_csel", [128, 33])
    t_ctxneg = din("t_ctxneg", [1, 128])
    pt = din("pt", [1, 256], I32)
    cacheC = din("cacheC", [nrows, 512])
    cacheS = din("cacheS", [nrows, 512])
    t_iota = din("t_iota", [128, 1])
    t_new = din("t_new", [128, 1])
    t_tks = din("t_tks", [16, 64])

    yp = dout("yp", [NT * 128, D_MODEL])
    ys = dout("ys", [NS, D_MODEL])
    kvp = dout("kvp", [NT * 128, 1024])
    kvs = dout("kvs", [NS, 1024])
    winp = dout("winp", [512, 512])
    wins = dout("wins", [NS, 512, 512])
    poolp = dout("poolp", [15, 1024])
    pools = dout("pools", [NS, 15, 1024])
    if DEBUG:
        dbg_o = dout("dbg_o", [128, 8 * TOK], BF16)
        dbg_yp = dout("dbg_yp", [128, 8 * TOK], BF16)

    with ExitStack() as ges:
        P = Prog(nc, ges)
        idn = P.sb("idn", [128, 128], BF16); b_idn = P.buf("idn")
        idnf = P.sb("idnf", [128, 128], F32); b_idnf = P.buf("idnf")
        gates = P.sb("gates", [128, NTT, 24], F32); b_gates = P.buf("gates")
        MT = P.sb("MT", [128, 16, TOK], BF16); b_MT = P.bufs("MT", NTT)
        oT = MT[:, 0:8, :]
        ypT = MT[:, 8:16, :]
        P.memset("pool", MT[:, 0:8, 1024:TOK], 0.0, [b_MT[8]])
        ones_bf = P.sb("ones_bf", [128, 512], BF16); b_ones = P.buf("ones")
        qsT = P.sb("qsT", [128, 8, NS], BF16); b_qsT = P.buf("qsT")
        for t_, b_ in ((idn, b_idn), (idnf, b_idnf)):
            P.memset("pool", t_[:], 0.0, [b_])
            P.op("pool", lambda e, t_=t_: e.affine_select(out=t_[:], in_=t_[:], pattern=[[-1, 128]], compare_op=ALU.not_equal,
                                                           fill=1.0, base=0, channel_multiplier=1), [b_], [b_])
        P.memset("pool", ones_bf[:], 1.0, [b_ones])
        b_cp = P.buf("cp")
        P.dma("sp", wins[:, 0:511, :], st_win[:, 1:512, :], track=b_cp)
        P.dma("sp", pools[:, 0:14, :], st_pool.rearrange("(s r) c -> s r c", r=15)[:, 1:15, :], track=b_cp)

        pbank = [P.ps(f"pb{j}", [128, 512], F32) for j in range(8)]
        b_pb = P.bufs("pb", 8)

        def pbf(j):
            return pbank[j][:].bitcast(BF16)

        gpre = P.sb("gpre", [128, D_MODEL], F32); b_gpre = P.buf("gpre")
        cs = P.sb("cs", [128, 17, 32], F32); b_cs = P.buf("cs")
        P.dma("sp", gpre[:], g_pre.partition_broadcast(128), writes=[b_gpre])
        P.dma("sp", cs[:, 0:8, :], cs_c.rearrange("(t p) c -> p t c", p=128), writes=[b_cs])
        P.dma("sp", cs[:, 8:16, :], cs_o.rearrange("(t p) c -> p t c", p=128), writes=[b_cs])
        P.dma("sp", cs[:, 16, :], cs_s, writes=[b_cs])
        NWB = 2
        W = {}

        def alloc_ab(tag):
            W["hT"] = P.sb("hT" + tag, [128, 16, 10 * 128], BF16); W["b_hT"] = P.bufs("hT" + tag, 10)
            W["xt"] = P.sb("xt" + tag, [128, D_MODEL], F32); W["b_xt"] = P.buf("xt" + tag)
            W["ssum"] = P.sb("ssum" + tag, [128, 2], F32); W["b_ss"] = P.bufs("ss" + tag, 2)
            W["hb"] = P.sb("hb" + tag, [128, D_MODEL], BF16); W["b_hb"] = P.buf("hb" + tag)
            W["wbuf"] = [P.sb(f"wb{tag}{j}", [128, 16, 256], BF16) for j in range(NWB)]; W["b_wb"] = P.bufs("wb" + tag, NWB)
            W["stg"] = [P.sb(f"stg{tag}{j}", [128, 256], F32) for j in range(4)]; W["b_stg"] = P.bufs("stg" + tag, 4)
            W["zb"] = [P.sb(f"zb{tag}{j}", [128, 256], BF16) for j in range(2)]; W["b_zb"] = P.bufs("zb" + tag, 2)
            W["rt"] = P.sb("rt" + tag, [128, 4, 2, HALF], F32); W["b_rt"] = P.buf("rt" + tag)

        pes0 = ExitStack()
        P.es = pes0
        alloc_ab("p")
        ctr = {"nrm": 0, "tr": 0, "pm": 0, "wb": 0, "stg": 0, "zb": 0}

        def rms_to_hT(src_ap, nrows, slot, gain, b_gain):
            hT, b_hT, xt, b_xt, ssum, b_ss, hb, b_hb = W["hT"], W["b_hT"], W["xt"], W["b_xt"], W["ssum"], W["b_ss"], W["hb"], W["b_hb"]
            if nrows < 128:
                P.memset("pool", xt[:], 0.0, [b_xt])
            P.dma("sp", xt[0:nrows, :], src_ap, writes=[b_xt])
            j = ctr["nrm"] % 2; ctr["nrm"] += 1
            s1 = ssum[:, j:j + 1]
            P.act(hb[:], xt[:], AF.Square, [b_xt], [b_hb, b_ss[j]], accum_out=s1)
            P.ts("dve", s1, s1, 1.0 / D_MODEL, EPS, ALU.mult, ALU.add, [b_ss[j]], [b_ss[j]])
            P.act(s1, s1, AF.Sqrt, [b_ss[j]], [b_ss[j]])
            P.op("dve", lambda e: e.reciprocal(out=s1, in_=s1), [b_ss[j]], [b_ss[j]])
            P.stt("dve", hb[:], xt[:], s1, gain[:], ALU.mult, ALU.mult, [b_xt, b_ss[j], b_gain], [b_hb])
            for half in range(2):
                k = 6 + ctr["tr"] % 2; ctr["tr"] += 1
                for c in range(8):
                    cc = half * 8 + c
                    P.tr(pbf(k)[:, c * 128:(c + 1) * 128], hb[:, cc * 128:(cc + 1) * 128], idn[:], [b_hb, b_idn], [b_pb[k]])
                P.copy("act", hT[:, half * 8:(half + 1) * 8, slot * 128:(slot + 1) * 128],
                       pbf(k).rearrange("p (c n) -> p c n", c=8), [b_pb[k]], [b_hT[slot]])

        def load_wblock(wsrc, c0, ncol):
            wbuf, b_wb = W["wbuf"], W["b_wb"]
            j = ctr["wb"] % NWB; ctr["wb"] += 1
            P.dma("pool", wbuf[j][:, :, 0:ncol], wsrc[:, c0:c0 + ncol].rearrange("(c p) n -> p c n", p=128), writes=[b_wb[j]])
            return j

        def rope(st, nh, ti, rd, wr):
            rt, b_rt = W["rt"], W["b_rt"]
            kv = st[:, 0:nh * 128].rearrange("p (h d) -> p h d", h=nh)
            x1 = kv[:, :, 0:HALF]
            x2 = kv[:, :, HALF:ROT]
            cosb = cs[:, ti, 0:HALF].unsqueeze(1).to_broadcast([128, nh, HALF])
            sinb = cs[:, ti, HALF:ROT].unsqueeze(1).to_broadcast([128, nh, HALF])
            r4 = rt[:, :, 0:nh, :]
            P.tt("dve", r4[:, 0], x1, cosb, ALU.mult, rd + [b_cs], [b_rt])
            P.tt("dve", r4[:, 1], x2, sinb, ALU.mult, rd + [b_cs], [b_rt])
            P.tt("dve", r4[:, 2], x1, sinb, ALU.mult, rd + [b_cs], [b_rt])
            P.tt("dve", r4[:, 3], x2, cosb, ALU.mult, rd + [b_cs], [b_rt])
            P.tt("dve", x1, r4[:, 0], r4[:, 1], ALU.subtract, [b_rt], wr)
            P.tt("dve", x2, r4[:, 2], r4[:, 3], ALU.add, [b_rt], wr)


        def proj_block(kind, c0, ncol, tiles):
            hT, b_hT, wbuf, b_wb, stg, b_stg, zb, b_zb = W["hT"], W["b_hT"], W["wbuf"], W["b_wb"], W["stg"], W["b_stg"], W["zb"], W["b_zb"]
            wj = load_wblock(w_in, c0, ncol)
            for slot, ti in tiles:
                pj = ctr["pm"] % 4; ctr["pm"] += 1
                for c in range(16):
                    P.mm(pbank[pj][:, 0:ncol], hT[:, c, slot * 128:(slot + 1) * 128], wbuf[wj][:, c, 0:ncol], c == 0, c == 15,
                         [b_hT[slot], b_wb[wj]], [b_pb[pj]])
                if kind == "g":
                    P.act(gates[:, ti - 8, :], pbank[pj][:, 0:24], AF.Sigmoid, [b_pb[pj]], [b_gates])
                    continue
                sj = ctr["stg"] % 4; ctr["stg"] += 1
                st = stg[sj]
                P.copy("act", st[:], pbank[pj][:, 0:256], [b_pb[pj]], [b_stg[sj]])
                own = 8 <= ti < 16
                smp = ti == 16
                if kind[0] in "qk":
                    rope(st, 2, ti, [b_stg[sj]], [b_stg[sj]])
                if kind in ("kc", "vc", "ks", "vs"):
                    oc = {"kc": 0, "vc": 256, "ks": 512, "vs": 768}[kind]
                    if own:
                        P.dma("sp", kvp[(ti - 8) * 128:(ti - 7) * 128, oc:oc + 256], st[:], reads=[b_stg[sj]])
                    elif smp:
                        P.dma("sp", kvs[:, oc:oc + 256], st[0:NS, :], reads=[b_stg[sj]])
                elif kind in ("kw", "vw"):
                    oc = 0 if kind == "kw" else 256
                    if 12 <= ti < 16:
                        P.dma("sp", winp[(ti - 12) * 128:(ti - 11) * 128, oc:oc + 256], st[:], reads=[b_stg[sj]])
                    elif smp:
                        P.dma("sp", wins[:, 511, oc:oc + 256], st[0:NS, :], reads=[b_stg[sj]])
                zj = ctr["zb"] % 2; ctr["zb"] += 1
                z = zb[zj]
                P.copy("dve", z[:], st[:], [b_stg[sj]], [b_zb[zj]])
                if kind[0] == "q":
                    qi = int(kind[1])
                    k = 6 + ctr["tr"] % 2; ctr["tr"] += 1
                    for h in range(2):
                        P.tr(pbf(k)[:, h * 128:(h + 1) * 128], z[:, h * 128:(h + 1) * 128], idn[:], [b_zb[zj], b_idn], [b_pb[k]])
                    P.copy("act", QT[:, 2 * qi:2 * qi + 2, (ti - 8) * 128:(ti - 7) * 128],
                           pbf(k)[:, 0:256].rearrange("p (c n) -> p c n", c=2), [b_pb[k]], [b_QT[ti - 8]])
                    if smp:
                        P.copy("act", qsT[:, 2 * qi:2 * qi + 2, :],
                               pbf(k)[:, 0:256].rearrange("p (c n) -> p c n", c=2)[:, :, 0:NS], [b_pb[k]], [b_qsT])
                elif kind[0] == "k" or kind == "vc":
                    if smp:
                        continue
                    x = kind[1]
                    kt = ti - 4 if x == "w" else ti
                    dst, bdst = (VCT, b_VCT) if kind == "vc" else (KT[x], b_KT[x])
                    k = 6 + ctr["tr"] % 2; ctr["tr"] += 1
                    for h in range(2):
                        P.tr(pbf(k)[:, h * 128:(h + 1) * 128], z[:, h * 128:(h + 1) * 128], idn[:], [b_zb[zj], b_idn], [b_pb[k]])
                    P.copy("act", dst[:, :, kt * 128:(kt + 1) * 128],
                           pbf(k)[:, 0:256].rearrange("p (c n) -> p c n", c=2), [b_pb[k]], [bdst[kt]])
                if kind in ("vs", "vw") and not smp:
                    x = kind[1]
                    kt = ti - 4 if x == "w" else ti
                    dst, bdst = (VS, b_VS) if x == "s" else (VW, b_VW)
                    P.copy("pool", dst[:, kt, :, 0:128], z[:].rearrange("p (g d) -> p g d", g=2), [b_zb[zj]], [bdst[kt]])

        COL = {"q0": 0, "q1": 256, "q2": 512, "q3": 768, "kc": 1024, "vc": 1280, "ks": 1536, "vs": 1792,
               "kw": 2048, "vw": 2304, "g": 2560}

        P.mute = KSKIP
        rms_to_hT(xc[7 * 128:8 * 128, :], 128, 0, gpre, b_gpre)
        for i in range(NT):
            rms_to_hT(xo[i * 128:(i + 1) * 128, :], 128, 1 + i, gpre, b_gpre)
        rms_to_hT(xs, NS, 9, gpre, b_gpre)

        pes = ExitStack()
        P.es = pes
        uT = P.sb("uT", [128, 1280], F32); b_uT = P.buf("uT")
        T0 = P.sb("T0", [128, 1280], F32); b_T0 = P.buf("T0")
        T1 = P.sb("T1", [128, 1280], F32); b_T1 = P.buf("T1")
        dT = P.sb("dT", [128, 2, TOK], BF16); b_dT = P.bufs("dT", 2)
        pw = P.sb("pw", [128, 4, 2, 256], BF16); b_pw = P.buf("pw")
        psc = P.sb("psc", [128, 8], F32); b_psc = P.buf("psc")
        pinv_t = P.sb("pinv_t", [128, 4, 16], F32); b_pinv = P.buf("pinv")
        hst = P.sb("hst", [120, 128], F32); b_hst = P.buf("hst")
        histT = P.sb("histT", [128, 2, 240], F32); b_histT = P.buf("histT")
        red = P.sb("red", [128, 16], F32); b_red = P.buf("red")
        tmp16 = P.sb("tmp16", [128, 16], F32); b_tmp16 = P.buf("tmp16")
        P.dma("pool", pw[:], pool_w.rearrange("g (c p) e -> p g c e", p=128), writes=[b_pw])
        P.dma("sp", psc[:], pool_sc.rearrange("(c p) -> p c", p=128), writes=[b_psc], allow_slow_non_contiguous=True)
        P.dma("sp", pinv_t[:], pinv.rearrange("p (g t) -> p g t", g=4), writes=[b_pinv])
        P.memset("pool", dT[:], 0.0, b_dT)
        tokblocks = [(0, 512), (512, 512), (1024, 256)]
        hT, b_hT, wbuf, b_wb = W["hT"], W["b_hT"], W["wbuf"], W["b_wb"]
        for g in range(4):
            wj = load_wblock(w_in, 2584 + g * 256, 256)
            w = 2 << g
            for cc in range(2):
                ch0 = g * 256 + cc * 128
                for hhalf in range(2):
                    P.dma("sp", hst[:, 0:128], st_pool[hhalf * 120:(hhalf + 1) * 120, ch0:ch0 + 128], writes=[b_hst])
                    k = 6 + ctr["tr"] % 2; ctr["tr"] += 1
                    P.tr(pbank[k][:, 0:120], hst[:, 0:128], idnf[0:120, 0:120], [b_hst, b_idnf], [b_pb[k]])
                    P.copy("act", histT[:, cc, hhalf * 120:(hhalf + 1) * 120], pbank[k][:, 0:120], [b_pb[k]], [b_histT])
            for cc in range(2):
                chunk = g * 2 + cc
                for (t0, tn) in tokblocks:
                    pj = ctr["pm"] % 4; ctr["pm"] += 1
                    for c in range(16):
                        P.mm(pbank[pj][:, 0:tn], wbuf[wj][:, c, cc * 128:(cc + 1) * 128], hT[:, c, t0:t0 + tn], c == 0, c == 15,
                             b_hT + [b_wb[wj]], [b_pb[pj]])
                    P.copy("act", uT[:, t0:t0 + tn], pbank[pj][:, 0:tn], [b_pb[pj]], [b_uT])
                P.dma("sp", poolp[:, chunk * 128:(chunk + 1) * 128].rearrange("r c -> c r"), uT[:, 1137:1152], reads=[b_uT], allow_slow_non_contiguous=True)
                P.dma("sp", pools[:, 14, chunk * 128:(chunk + 1) * 128].rearrange("s c -> c s"), uT[:, 1152:1168], reads=[b_uT], allow_slow_non_contiguous=True)
                src, bsrc = uT, b_uT
                step = 1
                pp = [(T0, b_T0), (T1, b_T1)]
                k = 0
                while step < w:
                    dst, bdst = pp[k % 2]; k += 1
                    lo = 113 + 2 * step - 1
                    P.tt("pool", dst[:, lo:1152], src[:, lo:1152], src[:, lo - step:1152 - step], ALU.add, [bsrc], [bdst])
                    src, bsrc = dst, bdst
                    step *= 2
                P.stt("dve", dT[:, cc, 0:1024], src[:, 128:1152], 1.0 / w, uT[:, 128:1152], ALU.mult, ALU.subtract, [bsrc, b_uT], [b_dT[cc]])
                P.tt("dve", tmp16[:], src[:, 128:144], pinv_t[:, g, :], ALU.mult, [bsrc, b_pinv], [b_tmp16])
                P.tt("dve", dT[:, cc, 0:16], tmp16[:], uT[:, 128:144], ALU.subtract, [b_tmp16, b_uT], [b_dT[cc]])
                hv = histT[:, cc, :].rearrange("p (s r) -> p s r", r=15)[:, :, 16 - w:15]
                P.op("dve", lambda e, hv=hv: e.tensor_reduce(out=red[:], in_=hv, axis=AX.X, op=ALU.add), [b_histT], [b_red])
                P.tt("dve", red[:], red[:], uT[:, 1152:1168], ALU.add, [b_red, b_uT], [b_red])
                P.stt("dve", dT[:, cc, 1024:1040], red[:], 1.0 / w, uT[:, 1152:1168], ALU.mult, ALU.subtract, [b_red, b_uT], [b_dT[cc]])
            for ec in range(2):
                for (t0, tn) in [(0, 512), (512, 512), (1024, 128)]:
                    pj = ctr["pm"] % 4; ctr["pm"] += 1
                    for cc in range(2):
                        P.mm(pbank[pj][:, 0:tn], pw[:, g, cc, ec * 128:(ec + 1) * 128], dT[:, cc, t0:t0 + tn], cc == 0, cc == 1,
                             b_dT + [b_pw], [b_pb[pj]])
                    P.act(ypT[:, g * 2 + ec, t0:t0 + tn], pbank[pj][:, 0:tn], AF.Copy, [b_pb[pj], b_psc], b_MT[t0 // 128:(t0 + tn) // 128],
                          scale=psc[:, g * 2 + ec:g * 2 + ec + 1])
        P.barrier(); P.emit()
        pes.close()
        pes0.close()
        P.es = ges

        P.mute = False
        aes = ExitStack()
        P.es = aes
        QT = P.sb("QT", [128, 8, TOK], BF16); b_QT = P.bufs("QT", NTT)
        KT = {"c": P.sb("KTc", [128, 2, 2048], BF16), "s": P.sb("KTs", [128, 2, 2048], BF16), "w": P.sb("KTw", [128, 2, 1536], BF16)}
        b_KT = {x: P.bufs("KT" + x, 16) for x in "csw"}
        VCT = P.sb("VCT", [128, 2, 2048], BF16); b_VCT = P.bufs("VCT", 16)
        VS = P.sb("VS", [128, 16, 2, 129], BF16); b_VS = P.bufs("VS", 16)
        VW = P.sb("VW", [128, 12, 2, 129], BF16); b_VW = P.bufs("VW", 12)
        P.memset("pool", VS[:, :, :, 128:129], 1.0, b_VS)
        P.memset("pool", VW[:, :, :, 128:129], 1.0, b_VW)

        bes = ExitStack()
        P.es = bes
        alloc_ab("b")
        for i in range(NT):
            rms_to_hT(xo[i * 128:(i + 1) * 128, :], 128, 1 + i, gpre, b_gpre)
        rms_to_hT(xs, NS, 9, gpre, b_gpre)
        own_tiles = [(1 + i, 8 + i) for i in range(NT)] + [(9, 16)]
        for kind in ("q0", "q1", "q2", "q3", "kc", "vc", "ks", "vs"):
            proj_block(kind, COL[kind], 256, own_tiles)
        proj_block("kw", COL["kw"], 256, own_tiles)
        proj_block("vw", COL["vw"], 256, own_tiles)
        proj_block("g", COL["g"], 24, own_tiles)
        for i in range(NT):
            rms_to_hT(xc[i * 128:(i + 1) * 128, :], 128, i, gpre, b_gpre)
        ctx_tiles = [(i, i) for i in range(NT)]
        for kind in ("kc", "vc", "ks", "vs"):
            proj_block(kind, COL[kind], 256, ctx_tiles)
        proj_block("kw", COL["kw"], 256, ctx_tiles[4:])
        proj_block("vw", COL["vw"], 256, ctx_tiles[4:])
        P.barrier(); P.emit()
        bes.close()
        P.es = aes

        P.mute = KSKIP
        des = ExitStack()
        P.es = des
        KcT = P.sb("KcT", [128, 2, 128], BF16); b_KcT = P.buf("KcT")
        VcA = P.sb("VcA", [128, 2, 161], BF16); b_VcA = P.buf("VcA")
        csel = P.sb("csel", [128, 33], BF16); b_csel = P.buf("csel")
        P.dma("pool", csel[:], t_csel, writes=[b_csel])
        def cmp_setup(tag):
            w1 = [P.sb(f"w1{tag}{i}", [128, 32, 128], BF16) for i in range(2)]; b_w1 = P.bufs("w1" + tag, 2)
            w2 = [P.sb(f"w2{tag}{i}", [128, 128], BF16) for i in range(2)]; b_w2 = P.bufs("w2" + tag, 2)
            posT = [P.sb(f"posT{tag}{i}", [128, 32], BF16) for i in range(2)]; b_posT = P.bufs("posT" + tag, 2)
            b1 = P.sb("b1" + tag, [128, 2], F32); b_b1 = P.buf("b1" + tag)
            gx = P.sb("gx" + tag, [128, 512], F32); b_gx = P.buf("gx" + tag)
            gt_ = P.sb("gt" + tag, [128, 512], F32); b_gt = P.buf("gt" + tag)
            gs_ = P.sb("gs" + tag, [128, 512], F32); b_gs = P.buf("gs" + tag)
            gl = P.sb("gl" + tag, [128, 512], BF16); b_gl = P.buf("gl" + tag)
            for i in range(2):
                P.dma("pool", w1[i][:], cw1[i].rearrange("(j p) h -> p j h", p=128), writes=[b_w1[i]])
                P.dma("pool", w2[i][:], cw2[i], writes=[b_w2[i]])
                P.dma("pool", posT[i][:], cpos[i].rearrange("j d -> d j"), writes=[b_posT[i]], allow_slow_non_contiguous=True)
                for j in range(32):
                    P.mm(pbank[0][:, i:i + 1], w1[i][:, j, :], posT[i][:, j:j + 1], (i == 0 and j == 0), j == 31, [b_w1[i], b_posT[i]], [b_pb[0]])
            P.copy("dve", b1[:], pbank[0][:, 0:2], [b_pb[0]], [b_b1])

            def gelu_hidden(pj, n, i):
                P.act(gx[:, 0:n], pbank[pj][:, 0:n], AF.Identity, [b_pb[pj], b_b1], [b_gx], bias=b1[:, i:i + 1])
                P.tt("dve", gt_[:, 0:n], gx[:, 0:n], gx[:, 0:n], ALU.mult, [b_gx], [b_gt])
                P.ts("dve", gt_[:, 0:n], gt_[:, 0:n], 0.044715, 1.0, ALU.mult, ALU.add, [b_gt], [b_gt])
                P.tt("dve", gt_[:, 0:n], gt_[:, 0:n], gx[:, 0:n], ALU.mult, [b_gt, b_gx], [b_gt])
                P.act(gs_[:, 0:n], gt_[:, 0:n], AF.Sigmoid, [b_gt], [b_gs], scale=1.5957691216057308)
                P.tt("dve", gl[:, 0:n], gx[:, 0:n], gs_[:, 0:n], ALU.mult, [b_gx, b_gs], [b_gl])
            return w1, b_w1, w2, b_w2, gl, b_gl, gelu_hidden

        w1, b_w1, w2, b_w2, gl, b_gl, gelu_hidden = cmp_setup("p")

        P.memset("pool", VcA[:], 0.0, [b_VcA])
        for i, (srcT, bsrc) in enumerate(((KT["c"], b_KT["c"]), (VCT, b_VCT))):
            for j in range(32):
                rhs = srcT[:, :, j:j + 16 * 126 + 1:16]
                P.mm(pbank[1][:, 0:254].rearrange("p (g n) -> p g n", g=2), w1[i][:, j, :], rhs, j == 0, j == 31, bsrc + [b_w1[i]], [b_pb[1]])
            gelu_hidden(1, 254, i)
            if i == 0:
                P.mm(pbank[2][:, 0:254], w2[0][:], gl[:, 0:254], True, True, [b_gl, b_w2[0]], [b_pb[2]])
                P.copy("act", KcT[:, :, 0:127], pbank[2][:, 0:254].rearrange("p (g n) -> p g n", g=2), [b_pb[2]], [b_KcT])
            else:
                for g in range(2):
                    P.mm(pbank[2][0:127, g * 128:(g + 1) * 128], gl[:, g * 127:(g + 1) * 127], w2[1][:], g == 0, g == 1, [b_gl, b_w2[1]], [b_pb[2]])
                P.copy("act", VcA[0:127, :, 0:128], pbank[2][0:127, 0:256].rearrange("p (g d) -> p g d", g=2), [b_pb[2]], [b_VcA])
        for g in range(2):
            P.copy("dve", VcA[:, g, 128:161], csel[:], [b_csel, b_VcA], [b_VcA])

        tri4 = P.sb("tri4", [128, 4, 512], BF16); b_tri4 = P.buf("tri4")
        wband = P.sb("wband", [128, 8, 512], BF16); b_wband = P.buf("wband")
        cmpb = P.sb("cmpb", [128, 1024], BF16); b_cmpb = P.buf("cmpb")
        tk = P.sb("tk", [128, 3, 8, 32], F32); b_tk = P.buf("tk")
        esel = P.sb("esel", [32, 16, 128], BF16); b_esel = P.buf("esel")
        ctxneg = P.sb("ctxneg", [1, 128], BF16); b_ctxneg = P.buf("ctxneg")
        P.dma("pool", tri4[:], t_tri4.rearrange("p (r q) -> p r q", r=4), writes=[b_tri4])
        P.dma("pool", wband[:], t_wband.rearrange("p (r q) -> p r q", r=8), writes=[b_wband])
        P.dma("pool", cmpb[:], t_cmpb, writes=[b_cmpb])
        P.dma("sp", tk[:], t_tk.rearrange("p (a t j) -> p a t j", a=3, t=8), writes=[b_tk])
        P.dma("pool", esel[:], t_esel.rearrange("j (k p) -> j k p", k=16), writes=[b_esel])
        P.dma("pool", ctxneg[:], t_ctxneg, writes=[b_ctxneg])
        ET = [P.sb(f"ET{j}", [128, 512], BF16) for j in range(3)]; b_ET = P.bufs("ET", 3)
        oacc = [P.sb(f"oacc{j}", [128, 4, 4, 128], F32) for j in range(2)]; b_oacc = P.bufs("oacc", 2)
        obf = P.sb("obf", [128, 4, 512], BF16); b_obf = P.buf("obf")
        imp = P.sb("imp", [128, 4, 32], F32); b_imp = P.buf("imp")
        sc_ = P.sb("sc", [128, 4, 32], F32); b_sc = P.buf("sc")
        scr = P.sb("scr", [128, 32], F32); b_scr = P.buf("scr")
        m8 = P.sb("m8", [128, 2, 8], F32); b_m8 = P.buf("m8")
        sbias = P.sb("sbias", [128, 4, 32], F32); b_sbias = P.buf("sbias")
        sbT = P.sb("sbT", [32, 512], BF16); b_sbT = P.buf("sbT")
        rsum = P.sb("rsum", [128, 8], F32); b_rsum = P.buf("rsum")
        actr = {"s": 0, "et": 0, "o": 0, "r": 0}

        def attend(h, qb, keytiles, br, oa, b_oa, first):
            g, hl = h // 4, h % 4
            qcols = slice(qb * 512, (qb + 1) * 512)
            ob = 2 + 2 * (actr["o"] % 2); actr["o"] += 1
            width = keytiles[0]["v"].shape[-1]
            started = [False, False]
            last = {}
            for ki, kt_ in enumerate(keytiles):
                for qt in kt_["qts"]:
                    last[qt] = ki
            for ki, kt_ in enumerate(keytiles):
                sj = actr["s"] % 2; actr["s"] += 1
                nk = kt_["nk"]
                nmm = 1 + len(kt_["masks"])
                P.mm(pbank[sj][0:nk, :], kt_["kT"], QT[:, h, qcols], True, nmm == 1, [kt_["bk"]] + b_QT[4 * qb:4 * qb + 4], [b_pb[sj]])
                for mi, (ml, mr, mb) in enumerate(kt_["masks"]):
                    P.mm(pbank[sj][0:nk, :], ml, mr, False, mi == nmm - 2, mb, [b_pb[sj]])
                ej = actr["et"] % 3; actr["et"] += 1
                P.act(ET[ej][0:nk, :], pbank[sj][0:nk, :], AF.Exp, [b_pb[sj]], [b_ET[ej]], scale=SCALE)
                for qt in kt_["qts"]:
                    bk_ = ob + qt // 2
                    reg = pbank[bk_][:, (qt % 2) * width:(qt % 2 + 1) * width]
                    st_flag = not started[qt // 2]
                    started[qt // 2] = True
                    P.mm(reg, ET[ej][0:nk, qt * 128:(qt + 1) * 128], kt_["v"], st_flag, last[qt] == ki, [b_ET[ej], kt_["bv"]], [b_pb[bk_]])
            for qt in range(4):
                bk_ = ob + qt // 2
                reg = pbank[bk_][:, (qt % 2) * width:(qt % 2 + 1) * width]
                rj = actr["r"] % 8; actr["r"] += 1
                r1 = rsum[:, rj:rj + 1]
                P.ts("dve", r1, reg[:, 128:129], 1e-30, None, ALU.max, None, [b_pb[bk_]], [b_rsum])
                P.op("dve", lambda e, r1=r1: e.reciprocal(out=r1, in_=r1), [b_rsum], [b_rsum])
                if br == 0:
                    P.stt("dve", imp[:, qt, :], reg[:, 129:161], r1, imp[:, qt, :], ALU.mult, ALU.add, [b_pb[bk_], b_rsum, b_imp], [b_imp])
                tile_i = 4 * qb + qt
                P.tt("dve", r1, r1, gates[:, tile_i, h * 3 + br:h * 3 + br + 1], ALU.mult, [b_rsum, b_gates], [b_rsum])
                if first:
                    P.act(oa[:, qt, hl, :], reg[:, 0:128], AF.Copy, [b_pb[bk_], b_rsum], [b_oa], scale=r1)
                else:
                    P.stt("dve", oa[:, qt, hl, :], reg[:, 0:128], r1, oa[:, qt, hl, :], ALU.mult, ALU.add, [b_pb[bk_], b_rsum, b_oa], [b_oa])

        it = 0
        for qb in range(2):
            for g in range(2):
                oa, b_oa = oacc[it % 2], b_oacc[it % 2]
                it += 1
                P.memset("pool", imp[:], 0.0, [b_imp])
                for hl in range(4):
                    h = g * 4 + hl
                    ktl = [dict(kT=KcT[:, g, 0:127], bk=b_KcT, v=VcA[0:127, g, :], bv=b_VcA, nk=127,
                                masks=[(idn[0:127, 0:127], cmpb[0:127, qb * 512:(qb + 1) * 512], [b_idn, b_cmpb])], qts=[0, 1, 2, 3])]
                    attend(h, qb, ktl, 0, oa, b_oa, True)
                P.tt("dve", sc_[:], imp[:], tk[:, 0, 4 * qb:4 * qb + 4, :], ALU.mult, [b_imp, b_tk], [b_sc])
                P.tt("dve", sc_[:], sc_[:], tk[:, 1, 4 * qb:4 * qb + 4, :], ALU.add, [b_sc, b_tk], [b_sc])
                for qt in range(4):
                    P.op("dve", lambda e, qt=qt: e.max(out=m8[:, 0, :], in_=sc_[:, qt, :]), [b_sc], [b_m8])
                    P.op("dve", lambda e, qt=qt: e.match_replace(out=scr[:], in_to_replace=m8[:, 0, :], in_values=sc_[:, qt, :], imm_value=-2.0), [b_sc, b_m8], [b_scr])
                    P.op("dve", lambda e: e.max(out=m8[:, 1, :], in_=scr[:]), [b_scr], [b_m8])
                    P.ts("dve", sbias[:, qt, :], sc_[:, qt, :], m8[:, 1, 7:8], NEG, ALU.is_lt, ALU.mult, [b_sc, b_m8], [b_sbias])
                P.tt("dve", sbias[:], sbias[:], tk[:, 2, 4 * qb:4 * qb + 4, :], ALU.add, [b_sbias, b_tk], [b_sbias])
                for qt in range(4):
                    P.tr(pbank[6][0:32, qt * 128:(qt + 1) * 128], sbias[:, qt, :], idnf[:], [b_sbias, b_idnf], [b_pb[6]])
                P.copy("act", sbT[:], pbank[6][0:32, :], [b_pb[6]], [b_sbT])
                for hl in range(4):
                    h = g * 4 + hl
                    ktl = []
                    for kt in range(8 + 4 * qb + 4):
                        masks = [(esel[:, kt, :], sbT[:], [b_esel, b_sbT])]
                        qts = [0, 1, 2, 3]
                        r = kt - 8 - 4 * qb
                        if r >= 0:
                            masks.append((idn[:], tri4[:, r, :], [b_idn, b_tri4]))
                            qts = [q_ for q_ in range(4) if q_ >= r]
                        ktl.append(dict(kT=KT["s"][:, g, kt * 128:(kt + 1) * 128], bk=b_KT["s"][kt], v=VS[:, kt, g, :], bv=b_VS[kt], nk=128,
                                        masks=masks, qts=qts))
                    attend(h, qb, ktl, 1, oa, b_oa, False)
                for hl in range(4):
                    h = g * 4 + hl
                    ktl = []
                    for r in range(8):
                        wkt = 4 * qb + r
                        masks = [(idn[:], wband[:, r, :], [b_idn, b_wband])]
                        if wkt < 4:
                            masks.append((ctxneg[0:1, :], ones_bf[0:1, :], [b_ctxneg, b_ones]))
                        qts = [q_ for q_ in range(4) if 0 <= q_ + 4 - r <= 4]
                        ktl.append(dict(kT=KT["w"][:, g, wkt * 128:(wkt + 1) * 128], bk=b_KT["w"][wkt], v=VW[:, wkt, g, :], bv=b_VW[wkt], nk=128,
                                        masks=masks, qts=qts))
                    attend(h, qb, ktl, 2, oa, b_oa, False)
                P.copy("pool", obf[:], oa[:].rearrange("p q h d -> p q (h d)"), [b_oa], [b_obf])
                for qt in range(4):
                    for hl in range(4):
                        P.tr(pbf(7)[:, hl * 128:(hl + 1) * 128], obf[:, qt, hl * 128:(hl + 1) * 128], idn[:], [b_obf, b_idn], [b_pb[7]])
                    c0 = qb * 512 + qt * 128
                    P.copy("act", oT[:, g * 4:(g + 1) * 4, c0:c0 + 128], pbf(7)[:, 0:512].rearrange("p (h n) -> p h n", h=4), [b_pb[7]], [b_MT[4 * qb + qt]])
        if DEBUG:
            P.dma("sp", dbg_o.rearrange("p (h n) -> p h n", h=8), oT, reads=b_MT)
            P.dma("sp", dbg_yp.rearrange("p (h n) -> p h n", h=8), ypT, reads=b_MT)
        P.barrier(); P.emit()
        des.close(); aes.close()
        P.es = ges

        P.mute = KSTOP <= 0
        ses2 = ExitStack()
        P.es = ses2
        pt_i = P.sb("pt_i", [128, 256], I32); b_pti = P.buf("pti")
        pt_f = P.sb("pt_f", [128, 256], F32); b_ptf = P.buf("ptf")
        idx_i = P.sb("idx_i", [128, 256], I32); b_idx = P.buf("idx")
        iota_t = P.sb("iota_t", [128, 1], F32); b_iota = P.buf("iota")
        tnew = P.sb("tnew", [128, 1], F32); b_tnew = P.buf("tnew")
        tks = P.sb("tks", [16, 2, 32], F32); b_tks = P.buf("tks")
        P.dma("sp", pt_i[:], pt.partition_broadcast(128), writes=[b_pti])
        P.dma("sp", iota_t[:], t_iota, writes=[b_iota])
        P.dma("sp", tnew[:], t_new, writes=[b_tnew])
        P.dma("sp", tks[:], t_tks.rearrange("p (a j) -> p a j", a=2), writes=[b_tks])
        P.copy("dve", pt_f[:], pt_i[:], [b_pti], [b_ptf])
        P.ts("dve", pt_f[:], pt_f[:], 128.0, iota_t[:, 0:1], ALU.mult, ALU.add, [b_ptf, b_iota], [b_ptf])
        P.copy("dve", idx_i[:], pt_f[:], [b_ptf], [b_idx])
        csel2 = P.sb("csel2", [128, 33], BF16); b_csel2 = P.buf("csel2")
        esel2 = P.sb("esel2", [32, 16, 128], BF16); b_esel2 = P.buf("esel2")
        P.dma("pool", csel2[:], t_csel, writes=[b_csel2])
        P.dma("pool", esel2[:], t_esel.rearrange("j (k p) -> j k p", k=16), writes=[b_esel2])
        ones_f = P.sb("ones_f", [16, 128], F32); b_onesf = P.buf("onesf")
        P.memset("pool", ones_f[:], 1.0, [b_onesf])
        NSTG = 8
        stg2 = [P.sb(f"sst{j}", [128, 512], F32) for j in range(NSTG)]; b_stg2 = P.bufs("sst", NSTG)
        oaccs = P.sb("oaccs", [128, 128], F32); b_oaccs = P.buf("oaccs")
        tmpo = P.sb("tmpo", [128, 128], F32); b_tmpo = P.buf("tmpo")
        rC = P.sb("rC", [128, 128], F32); b_rC = P.buf("rC")
        gB = P.sb("gB", [128, 3, 128], F32); b_gB = P.buf("gB")
        Gm = P.sb("Gm", [16, 3, 16, 8], F32); b_Gm = P.buf("Gm")
        selb = P.sb("selb", [128, 17, 2, NS], F32); b_selb = P.buf("selb")
        sctr = {"st": 0, "cb": 0, "h": 0, "sc": 0}

        for br in range(3):
            gv = gates[0:NS, 8, :].rearrange("p (h b) -> p h b", b=3)[:, :, br]
            P.tt("dve", Gm[:, br], gv.unsqueeze(1).to_broadcast([NS, NS, 8]), idnf[0:NS, 0:NS].unsqueeze(2).to_broadcast([NS, NS, 8]),
                 ALU.mult, [b_gates, b_idnf], [b_Gm])
        for br in range(3):
            P.mm(pbank[0][:, br * 128:(br + 1) * 128], ones_f[:], Gm[:, br].rearrange("p s h -> p (s h)"), True, True, [b_onesf, b_Gm], [b_pb[0]])
        P.copy("act", gB[:], pbank[0][:, 0:384].rearrange("p (b n) -> p b n", b=3), [b_pb[0]], [b_gB])

        P.mute = P.mute or KSTOP <= 1
        p1 = ExitStack()
        P.es = p1
        w1, b_w1, w2, b_w2, gl, b_gl, gelu_hidden = cmp_setup("s")
        cbf = [P.sb(f"cbf{j}", [128, 2, 512], BF16) for j in range(2)]; b_cbf = P.bufs("cbf", 2)
        XT = P.sb("XTc", [128, 2, 2, 2048], BF16); b_XT = P.buf("XTc")
        KcS = P.sb("KcS", [128, 2, 128], BF16); b_KcS = P.buf("KcS")
        VcS = P.sb("VcS", [128, 2, 128], BF16); b_VcS = P.buf("VcS")
        ETc = P.sb("ETc", [128, 8], BF16); b_ETc = P.buf("ETc")
        rs8 = P.sb("rs8", [128, 8], F32); b_rs8 = P.buf("rs8")
        EnAll = P.sb("EnAll", [128, 8, NS], BF16); b_En = P.buf("EnAll")
        for s_ in range(NS):
            for pp in range(8):
                cj = sctr["cb"] % 2; sctr["cb"] += 1
                for k in range(2):
                    sj = sctr["st"] % NSTG; sctr["st"] += 1
                    col = s_ * 16 + pp * 2 + k
                    P.idma(stg2[sj][:], cacheC, idx_i[:, col:col + 1], nrows, reads=[b_idx], writes=[b_stg2[sj]])
                    P.copy("dve", cbf[cj][:, k, :], stg2[sj][:], [b_stg2[sj]], [b_cbf[cj]])
                k_ = 6 + ctr["tr"] % 2; ctr["tr"] += 1
                for kv in range(2):
                    for g in range(2):
                        for k in range(2):
                            c0 = ((kv * 2 + g) * 2 + k) * 128
                            P.tr(pbf(k_)[:, c0:c0 + 128], cbf[cj][:, k, (kv * 2 + g) * 128:(kv * 2 + g + 1) * 128], idn[:], [b_cbf[cj], b_idn], [b_pb[k_]])
                P.copy("act", XT[:, :, :, pp * 256:(pp + 1) * 256], pbf(k_).rearrange("p (kv g t) -> p kv g t", kv=2, g=2), [b_pb[k_]], [b_XT])
            for i in range(2):
                hb_ = sctr["h"] % 2; sctr["h"] += 1
                for j in range(32):
                    rhs = XT[:, i, :, j:j + 16 * 126 + 1:16]
                    P.mm(pbank[hb_][:, 0:254].rearrange("p (g n) -> p g n", g=2), w1[i][:, j, :], rhs, j == 0, j == 31, [b_XT, b_w1[i]], [b_pb[hb_]])
                gelu_hidden(hb_, 254, i)
                if i == 0:
                    P.mm(pbank[2][:, 0:254], w2[0][:], gl[:, 0:254], True, True, [b_gl, b_w2[0]], [b_pb[2]])
                    P.copy("act", KcS[:, :, 0:127], pbank[2][:, 0:254].rearrange("p (g n) -> p g n", g=2), [b_pb[2]], [b_KcS])
                else:
                    for g in range(2):
                        P.mm(pbank[2][0:127, g * 128:(g + 1) * 128], gl[:, g * 127:(g + 1) * 127], w2[1][:], g == 0, g == 1, [b_gl, b_w2[1]], [b_pb[2]])
                    P.copy("act", VcS[0:127, :, :], pbank[2][0:127, 0:256].rearrange("p (g d) -> p g d", g=2), [b_pb[2]], [b_VcS])
            for g in range(2):
                P.mm(pbank[3][0:127, g * 4:(g + 1) * 4], KcS[:, g, 0:127], qsT[:, g * 4:(g + 1) * 4, s_], g == 0, g == 1, [b_KcS, b_qsT], [b_pb[3]])
            P.act(ETc[0:127, :], pbank[3][0:127, 0:8], AF.Exp, [b_pb[3]], [b_ETc], scale=SCALE)
            for g in range(2):
                c0 = s_ * 8 + g * 4
                P.mm(pbank[4][:, c0:c0 + 4], VcS[0:127, g, :], ETc[0:127, g * 4:(g + 1) * 4], True, True, [b_VcS, b_ETc], [b_pb[4]])
            P.mm(pbank[5][:, s_ * 8:(s_ + 1) * 8], ones_bf[0:127, 0:128], ETc[0:127, 0:8], True, True, [b_ETc, b_ones], [b_pb[5]])
            P.op("dve", lambda e, s_=s_: e.reciprocal(out=rs8[:], in_=pbank[5][:, s_ * 8:(s_ + 1) * 8]), [b_pb[5]], [b_rs8])
            P.tt("dve", EnAll[0:127, :, s_], ETc[0:127, :], rs8[0:127, :], ALU.mult, [b_ETc, b_rs8], [b_En])
            P.mute = P.mute or KSTOP <= 2
        P.op("dve", lambda e: e.reciprocal(out=rC[:], in_=pbank[5][:, 0:128]), [b_pb[5]], [b_rC])
        P.tt("dve", rC[:], rC[:], gB[:, 0, :], ALU.mult, [b_rC, b_gB], [b_rC])
        P.tt("dve", oaccs[:], pbank[4][:, 0:128], rC[:], ALU.mult, [b_pb[4], b_rC], [b_oaccs])
        for g in range(2):
            for hl in range(4):
                P.mm(pbank[1][0:NS, g * 32:(g + 1) * 32], EnAll[0:127, g * 4 + hl, :], csel2[0:127, 1:33], (g == 0 and hl == 0), hl == 3,
                     [b_En, b_csel2], [b_pb[1]])
        scs = P.sb("scs", [16, 2, 32], F32); b_scs = P.buf("scs")
        sbs = P.sb("sbs", [16, 2, 32], F32); b_sbs = P.buf("sbs")
        scrs = P.sb("scrs", [16, 32], F32); b_scrs = P.buf("scrs")
        m8s = P.sb("m8s", [16, 2, 8], F32); b_m8s = P.buf("m8s")
        sbT2 = P.sb("sbT2", [32, 2 * NS], BF16); b_sbT2 = P.buf("sbT2")
        P.tt("dve", scs[:], pbank[1][0:NS, 0:64].rearrange("p (g j) -> p g j", g=2), tks[:, 0, :].unsqueeze(1).to_broadcast([NS, 2, 32]),
             ALU.mult, [b_pb[1], b_tks], [b_scs])
        P.tt("dve", scs[:], scs[:], tks[:, 1, :].unsqueeze(1).to_broadcast([NS, 2, 32]), ALU.add, [b_scs, b_tks], [b_scs])
        for g in range(2):
            P.op("dve", lambda e, g=g: e.max(out=m8s[:, 0, :], in_=scs[:, g, :]), [b_scs], [b_m8s])
            P.op("dve", lambda e, g=g: e.match_replace(out=scrs[:], in_to_replace=m8s[:, 0, :], in_values=scs[:, g, :], imm_value=-2.0), [b_scs, b_m8s], [b_scrs])
            P.op("dve", lambda e: e.max(out=m8s[:, 1, :], in_=scrs[:]), [b_scrs], [b_m8s])
            P.ts("dve", sbs[:, g, :], scs[:, g, :], m8s[:, 1, 6:7], NEG, ALU.is_lt, ALU.mult, [b_scs, b_m8s], [b_sbs])
        for g in range(2):
            P.tr(pbank[6][0:32, g * NS:(g + 1) * NS], sbs[:, g, :], idnf[0:NS, 0:NS], [b_sbs, b_idnf], [b_pb[6]])
        P.copy("act", sbT2[:], pbank[6][0:32, 0:2 * NS], [b_pb[6]], [b_sbT2])
        for j in range(16):
            P.mm(pbank[7][:, j * 32:(j + 1) * 32], esel2[:, j, :], sbT2[:], True, True, [b_esel2, b_sbT2], [b_pb[7]])
        P.copy("act", selb[:, 0:16, :, :], pbank[7][:].rearrange("p (j g s) -> p j g s", j=16, g=2), [b_pb[7]], [b_selb])
        P.copy("dve", selb[:, 16, :, :], tnew[:, 0:1].unsqueeze(2).to_broadcast([128, 2, NS]), [b_tnew], [b_selb])
        P.barrier(); P.emit()
        p1.close()

        P.mute = P.mute or KSTOP <= 3
        p2 = ExitStack()
        P.es = p2
        cbk = [P.sb(f"cbk{j}", [128, 2, 256], BF16) for j in range(2)]; b_cbk = P.bufs("cbk", 2)
        KsT = P.sb("KsT2", [128, 2, 17 * 128], BF16); b_KsT = P.buf("KsT2")
        Vs2 = P.sb("Vs2", [128, 17, 256], BF16); b_Vs2 = P.buf("Vs2")
        nw = P.sb("nw", [128, 512], F32); b_nw = P.buf("nw")
        Wf = P.sb("Wf", [128, 4, 512], F32); b_Wf = P.buf("Wf")
        Wb = P.sb("Wb", [128, 4, 512], BF16); b_Wb = P.buf("Wb")
        KwT = P.sb("KwT", [128, 2, 512], BF16); b_KwT = P.buf("KwT")
        scm = P.sb("scm", [128, 136], F32); b_scm = P.buf("scm")
        ETs = P.sb("ETs", [128, 136], BF16); b_ETs = P.buf("ETs")
        ETw = P.sb("ETw", [128, 32], BF16); b_ETw = P.buf("ETw")
        P.memset("pool", nw[:], 0.0, [b_nw])
        for s_ in range(NS):
            P.dma("sp", Wf[:], wins[s_].rearrange("(c p) n -> p c n", p=128), writes=[b_Wf])
            P.dma("sp", nw[0:1, :], kvs[s_:s_ + 1, 512:1024], writes=[b_nw])
            P.copy("pool", Wb[:], Wf[:], [b_Wf], [b_Wb])
            k_ = 6 + ctr["tr"] % 2; ctr["tr"] += 1
            for g in range(2):
                for c in range(4):
                    c0 = (g * 4 + c) * 128
                    P.tr(pbf(k_)[:, c0:c0 + 128], Wb[:, c, g * 128:(g + 1) * 128], idn[:], [b_Wb, b_idn], [b_pb[k_]])
            P.copy("act", KwT[:], pbf(k_).rearrange("p (g t) -> p g t", g=2), [b_pb[k_]], [b_KwT])
            for pp in range(9):
                cj = sctr["cb"] % 2; sctr["cb"] += 1
                npg = 2 if pp < 8 else 1
                for k in range(npg):
                    pg = pp * 2 + k
                    if pp < 8:
                        sj = sctr["st"] % NSTG; sctr["st"] += 1
                        col = s_ * 16 + pg
                        P.idma(stg2[sj][:], cacheS, idx_i[:, col:col + 1], nrows, reads=[b_idx], writes=[b_stg2[sj]])
                        src_, bsrc_ = stg2[sj], b_stg2[sj]
                    else:
                        src_, bsrc_ = nw, b_nw
                    P.copy("dve", cbk[cj][:, k, :], src_[:, 0:256], [bsrc_], [b_cbk[cj]])
                    P.copy("pool", Vs2[:, pg, :], src_[:, 256:512], [bsrc_], [b_Vs2])
                k_ = 6 + ctr["tr"] % 2; ctr["tr"] += 1
                for g in range(2):
                    for k in range(npg):
                        c0 = (g * npg + k) * 128
                        P.tr(pbf(k_)[:, c0:c0 + 128], cbk[cj][:, k, g * 128:(g + 1) * 128], idn[:], [b_cbk[cj], b_idn], [b_pb[k_]])
                wdt = npg * 128
                P.copy("act", KsT[:, :, pp * 256:pp * 256 + wdt], pbf(k_)[:, 0:2 * wdt].rearrange("p (g t) -> p g t", g=2), [b_pb[k_]], [b_KsT])
            sb_ = sctr["sc"] % 2; sctr["sc"] += 1
            for j in range(17):
                for g in range(2):
                    c0 = (j * 2 + g) * 4
                    P.mm(pbank[sb_][:, c0:c0 + 4], KsT[:, g, j * 128:(j + 1) * 128], qsT[:, g * 4:(g + 1) * 4, s_], True, True, [b_KsT, b_qsT], [b_pb[sb_]])
            for c in range(4):
                for g in range(2):
                    c0 = 136 + (c * 2 + g) * 4
                    P.mm(pbank[sb_][:, c0:c0 + 4], KwT[:, g, c * 128:(c + 1) * 128], qsT[:, g * 4:(g + 1) * 4, s_], True, True, [b_KwT, b_qsT], [b_pb[sb_]])
            P.tt("dve", scm[:].rearrange("p (j g h) -> p j g h", j=17, g=2), pbank[sb_][:, 0:136].rearrange("p (j g h) -> p j g h", j=17, g=2),
                 selb[:, :, :, s_].unsqueeze(3).to_broadcast([128, 17, 2, 4]), ALU.add, [b_pb[sb_], b_selb], [b_scm])
            P.act(ETs[:], scm[:], AF.Exp, [b_scm], [b_ETs], scale=SCALE)
            P.act(ETw[:], pbank[sb_][:, 136:168], AF.Exp, [b_pb[sb_]], [b_ETw], scale=SCALE)
            first = s_ == 0
            for j in range(17):
                for g in range(2):
                    c0 = s_ * 8 + g * 4
                    P.mm(pbank[2][:, c0:c0 + 4], Vs2[:, j, g * 128:(g + 1) * 128], ETs[:, (j * 2 + g) * 4:(j * 2 + g) * 4 + 4],
                         first and j == 0 and g == 0, j == 16, [b_Vs2, b_ETs], [b_pb[2]])
            for j in range(17):
                P.mm(pbank[3][:, s_ * 8:(s_ + 1) * 8], ones_bf[:, 0:128], ETs[:, j * 8:(j + 1) * 8], first and j == 0, j == 16, [b_ETs, b_ones], [b_pb[3]])
            for c in range(4):
                for g in range(2):
                    c0 = s_ * 8 + g * 4
                    P.mm(pbank[4][:, c0:c0 + 4], Wb[:, c, 256 + g * 128:256 + (g + 1) * 128], ETw[:, (c * 2 + g) * 4:(c * 2 + g) * 4 + 4],
                         first and c == 0 and g == 0, c == 3, [b_Wb, b_ETw], [b_pb[4]])
            for c in range(4):
                P.mm(pbank[5][:, s_ * 8:(s_ + 1) * 8], ones_bf[:, 0:128], ETw[:, c * 8:(c + 1) * 8], first and c == 0, c == 3, [b_ETw, b_ones], [b_pb[5]])
            P.mute = P.mute or KSTOP <= 4
        for (bn, bs, br) in ((2, 3, 1), (4, 5, 2)):
            P.op("dve", lambda e, bs=bs: e.reciprocal(out=rC[:], in_=pbank[bs][:, 0:128]), [b_pb[bs]], [b_rC])
            P.tt("dve", rC[:], rC[:], gB[:, br, :], ALU.mult, [b_rC, b_gB], [b_rC])
            P.tt("dve", tmpo[:], pbank[bn][:, 0:128], rC[:], ALU.mult, [b_pb[bn], b_rC], [b_tmpo])
            P.tt("dve", oaccs[:], oaccs[:], tmpo[:], ALU.add, [b_oaccs, b_tmpo], [b_oaccs])
        P.copy("act", MT[:, 0:8, 1024:1024 + NS], oaccs[:].rearrange("p (s h) -> p h s", h=8), [b_oaccs], [b_MT[8]])
        P.barrier(); P.emit()
        p2.close()
        ses2.close()
        P.es = ges

        P.mute = KSKIP
        fes = ExitStack()
        P.es = fes
        wo = P.sb("wo", [128, 16, D_MODEL], BF16); b_wo = P.bufs("wo", 4)
        for cb in range(4):
            P.dma("pool", wo[:, :, cb * 512:(cb + 1) * 512], w_o[:, cb * 512:(cb + 1) * 512].rearrange("(c p) n -> p c n", p=128), writes=[b_wo[cb]])
        gpost = P.sb("gpost", [128, D_MODEL], F32); b_gpost = P.buf("gpost")
        gmpre = P.sb("gmpre", [128, D_MODEL], F32); b_gmpre = P.buf("gmpre")
        P.dma("sp", gpost[:], g_post.partition_broadcast(128), writes=[b_gpost])
        P.dma("sp", gmpre[:], g_mpre.partition_broadcast(128), writes=[b_gmpre])
        xt2 = P.sb("xt2", [128, D_MODEL], F32); b_xt2 = P.buf("xt2")
        x1 = P.sb("x1", [128, D_MODEL], F32); b_x1 = P.buf("x1")
        tmpf = [P.sb(f"tmpf{j}", [128, 512], F32) for j in range(2)]; b_tmpf = P.bufs("tmpf", 2)
        sq2 = P.sb("sq2", [128, D_MODEL], BF16); b_sq2 = P.buf("sq2")
        ss4 = P.sb("ss4", [128, 8], F32); b_ss4 = P.buf("ss4")
        hb2 = P.sb("hb2", [128, D_MODEL], BF16); b_hb2 = P.buf("hb2")
        P.memset("pool", xt2[:], 0.0, [b_xt2])

        def rstd_from(col_in, col_out):
            P.ts("dve", ss4[:, col_out:col_out + 1], ss4[:, col_in:col_in + 1], 1.0 / D_MODEL, EPS, ALU.mult, ALU.add, [b_ss4], [b_ss4])
            P.act(ss4[:, col_out:col_out + 1], ss4[:, col_out:col_out + 1], AF.Sqrt, [b_ss4], [b_ss4])
            P.op("dve", lambda e: e.reciprocal(out=ss4[:, col_out:col_out + 1], in_=ss4[:, col_out:col_out + 1]), [b_ss4], [b_ss4])

        for i in range(NTT):
            nrow = 128 if i < NT else NS
            xsrc = xo[i * 128:(i + 1) * 128, :] if i < NT else xs
            ydst = yp[i * 128:(i + 1) * 128, :] if i < NT else ys
            for cb in range(4):
                for c in range(16):
                    P.mm(pbank[cb][:], MT[:, c, i * 128:(i + 1) * 128], wo[:, c, cb * 512:(cb + 1) * 512], c == 0, c == 15, [b_MT[i], b_wo[cb]], [b_pb[cb]])
            for cb in range(4):
                P.act(sq2[:, cb * 512:(cb + 1) * 512], pbank[cb][:], AF.Square, [b_pb[cb]], [b_sq2, b_ss4], accum_out=ss4[:, cb:cb + 1])
            P.op("dve", lambda e: e.tensor_reduce(out=ss4[:, 4:5], in_=ss4[:, 0:4], axis=AX.X, op=ALU.add), [b_ss4], [b_ss4])
            rstd_from(4, 4)
            P.dma("sp", xt2[0:nrow, :], xsrc, writes=[b_xt2])
            for cb in range(4):
                tj = cb % 2
                P.stt("dve", tmpf[tj][:], pbank[cb][:], ss4[:, 4:5], gpost[:, cb * 512:(cb + 1) * 512], ALU.mult, ALU.mult, [b_pb[cb], b_ss4, b_gpost], [b_tmpf[tj]])
                P.tt("pool", x1[:, cb * 512:(cb + 1) * 512], tmpf[tj][:], xt2[:, cb * 512:(cb + 1) * 512], ALU.add, [b_tmpf[tj], b_xt2], [b_x1])
            P.dma("sp", ydst, x1[0:nrow, :], reads=[b_x1])
            P.act(sq2[:], x1[:], AF.Square, [b_x1], [b_sq2, b_ss4], accum_out=ss4[:, 5:6])
            rstd_from(5, 6)
            P.stt("dve", hb2[:], x1[:], ss4[:, 6:7], gmpre[:], ALU.mult, ALU.mult, [b_x1, b_ss4, b_gmpre], [b_hb2])
            for half in range(2):
                k = 6 + half
                for c in range(8):
                    cc = half * 8 + c
                    P.tr(pbf(k)[:, c * 128:(c + 1) * 128], hb2[:, cc * 128:(cc + 1) * 128], idn[:], [b_hb2, b_idn], [b_pb[k]])
                P.copy("act", MT[:, half * 8:(half + 1) * 8, i * 128:(i + 1) * 128], pbf(k).rearrange("p (c n) -> p c n", c=8), [b_pb[k]], [b_MT[i]])
        P.barrier(); P.emit()
        fes.close()
        P.es = ges

        mes = ExitStack()
        P.es = mes
        yacc = P.sb("yacc", [128, NTT, D_MODEL], F32); b_yacc = P.bufs("yacc", NTT)
        ses = ExitStack()
        P.es = ses
        wu = [P.sb(f"wu{j}", [128, 16, 512], BF16) for j in range(2)]; b_wu = P.bufs("wu", 2)
        wd = [P.sb(f"wd{j}", [128, 4, D_MODEL], BF16) for j in range(2)]; b_wd = P.bufs("wd", 2)
        aT = [P.sb(f"aT{j}", [128, 4, TOK], BF16) for j in range(2)]; b_aT = P.bufs("aT", 2)
        NSLAB = D_FF // 512
        upc = 0
        dnc = 0
        for sl in range(NSLAB):
            j = sl % 2
            P.dma("pool", wu[j][:], w_up[:, sl * 512:(sl + 1) * 512].rearrange("(c p) n -> p c n", p=128), writes=[b_wu[j]])
            P.dma("pool", wd[j][:], w_down[sl * 512:(sl + 1) * 512, :].rearrange("(b p) n -> p b n", p=128), writes=[b_wd[j]])
            for fb in range(4):
                for (t0, tn) in [(0, 512), (512, 512), (1024, 128)]:
                    pj = upc % 3; upc += 1
                    for c in range(16):
                        P.mm(pbank[pj][:, 0:tn], wu[j][:, c, fb * 128:(fb + 1) * 128], MT[:, c, t0:t0 + tn], c == 0, c == 15,
                             b_MT[t0 // 128:(t0 + tn) // 128] + [b_wu[j]], [b_pb[pj]])
                    P.act(aT[j][:, fb, t0:t0 + tn], pbank[pj][:, 0:tn], AF.Relu, [b_pb[pj]], [b_aT[j]])
                P.tt("pool", aT[j][:, fb, :], aT[j][:, fb, :], aT[j][:, fb, :], ALU.mult, [b_aT[j]], [b_aT[j]])
            for i in range(NTT):
                for cb in range(4):
                    pj = 3 + dnc % 4; dnc += 1
                    for fb in range(4):
                        P.mm(pbank[pj][:], aT[j][:, fb, i * 128:(i + 1) * 128], wd[j][:, fb, cb * 512:(cb + 1) * 512], fb == 0, fb == 3, [b_aT[j], b_wd[j]], [b_pb[pj]])
                    ya = yacc[:, i, cb * 512:(cb + 1) * 512]
                    if sl == 0:
                        P.copy("dve", ya, pbank[pj][:], [b_pb[pj]], [b_yacc[i]])
                    else:
                        P.tt("dve", ya, pbank[pj][:], ya, ALU.add, [b_pb[pj], b_yacc[i]], [b_yacc[i]])
        P.barrier(); P.emit()
        ses.close()
        P.es = mes
        gmpost = P.sb("gmpost", [128, D_MODEL], F32); b_gmpost = P.buf("gmpost")
        P.dma("sp", gmpost[:], g_mpost.partition_broadcast(128), writes=[b_gmpost])
        x1r = [P.sb(f"x1r{j}", [128, D_MODEL], F32) for j in range(2)]; b_x1r = P.bufs("x1r", 2)
        tf = [P.sb(f"tf{j}", [128, D_MODEL], F32) for j in range(2)]; b_tf = P.bufs("tf", 2)
        sq3 = P.sb("sq3", [128, D_MODEL], BF16); b_sq3 = P.buf("sq3")
        ss5 = P.sb("ss5", [128, 2], F32); b_ss5 = P.bufs("ss5", 2)
        for i in range(NTT):
            j = i % 2
            nrow = 128 if i < NT else NS
            ydst = yp[i * 128:(i + 1) * 128, :] if i < NT else ys
            P.dma("sp", x1r[j][0:nrow, :], ydst, writes=[b_x1r[j]])
            s1 = ss5[:, j:j + 1]
            P.act(sq3[:], yacc[:, i, :], AF.Square, [b_yacc[i]], [b_sq3, b_ss5[j]], accum_out=s1)
            P.ts("dve", s1, s1, 1.0 / D_MODEL, EPS, ALU.mult, ALU.add, [b_ss5[j]], [b_ss5[j]])
            P.act(s1, s1, AF.Sqrt, [b_ss5[j]], [b_ss5[j]])
            P.op("dve", lambda e, s1=s1: e.reciprocal(out=s1, in_=s1), [b_ss5[j]], [b_ss5[j]])
            P.stt("dve", tf[j][:], yacc[:, i, :], s1, gmpost[:], ALU.mult, ALU.mult, [b_yacc[i], b_ss5[j], b_gmpost], [b_tf[j]])
            P.tt("pool", tf[j][0:nrow, :], tf[j][0:nrow, :], x1r[j][0:nrow, :], ALU.add, [b_tf[j], b_x1r[j]], [b_tf[j]])
            P.dma("sp", ydst, tf[j][0:nrow, :], reads=[b_tf[j]])
        P.barrier(); P.emit()
        mes.close()
    return nc


def _rope_table(pos):
    inv = np.power(np.float32(THETA), -np.arange(HALF, dtype=np.float32) * np.float32(2.0 / ROT)).astype(np.float32)
    ang = pos.astype(np.float32)[:, None] * inv[None, :]
    return np.concatenate([np.cos(ang), np.sin(ang)], axis=1).astype(np.float32)


def _tables(hf):
    p = np.arange(128)[:, None]
    j = np.arange(128)[None, :]
    tri4 = np.zeros((128, 4, 4, 128), np.float32)
    for r in range(4):
        for c in range(4):
            if c < r:
                tri4[:, r, c, :] = NEG
            elif c == r:
                tri4[:, r, c, :] = np.where(p > j, NEG, 0.0)
    wband = np.zeros((128, 8, 4, 128), np.float32)
    for r in range(8):
        for c in range(4):
            rel = c + 4 - r
            if rel < 0 or rel > 4:
                wband[:, r, c, :] = NEG
            elif rel == 0:
                wband[:, r, c, :] = np.where(p > j, NEG, 0.0)
            elif rel == 4:
                wband[:, r, c, :] = np.where(p <= j, NEG, 0.0)
    n = np.arange(128)[:, None]
    t = np.arange(1024)[None, :]
    bad = (16 * n + 31 > 1024 + t)
    if hf == 0:
        bad = bad | (n < 64)
    cmpb = np.where(bad, NEG, 0.0).astype(np.float32)
    tk = np.zeros((128, 3, 8, 32), np.float32)
    jj = np.arange(32)[None, :]
    for i in range(8):
        jt = 16 + 2 * i + (np.arange(128)[:, None] >= 64)
        valid = jj <= jt
        if hf == 0:
            valid = valid & (jj >= 16)
        first = 0 if hf == 1 else 16
        forced = valid & ((jj == jt) | (jj == jt - 1) | (jj == first))
        tk[:, 0, i, :] = (valid & ~forced).astype(np.float32)
        tk[:, 1, i, :] = np.where(forced, 1e4, np.where(valid, 0.0, -1.0))
        tk[:, 2, i, :] = np.where(valid, 0.0, NEG)
    esel = np.zeros((32, 16, 128), np.float32)
    for kt in range(16):
        esel[2 * kt, kt, 0:64] = 1.0
        esel[2 * kt + 1, kt, 64:128] = 1.0
    csel = np.zeros((128, 33), np.float32)
    csel[0:127, 0] = 1.0
    nn = np.arange(127)[:, None]
    s_ = np.arange(32)[None, :]
    csel[0:127, 1:33] = ((16 * nn < 64 * s_ + 64) & (16 * nn + 32 > 64 * s_)).astype(np.float32)
    ctxneg = np.full((1, 128), NEG * (1 - hf), np.float32)
    pinv = np.zeros((128, 4, 16), np.float32)
    for g in range(4):
        w = 2 << g
        tp = hf * 1024 + np.arange(16)
        pinv[:, g, :] = (1.0 / np.minimum(w, tp + 1)).astype(np.float32)[None, :]
    return dict(t_tri4=tri4.reshape(128, -1), t_wband=wband.reshape(128, -1), t_cmpb=cmpb, t_tk=tk.reshape(128, -1),
                t_esel=esel.reshape(32, -1), t_csel=csel, t_ctxneg=ctxneg, pinv=pinv.reshape(128, -1))


_NC_CACHE = {}


def kernel(x_prompt, x_sample, cache_kv, state_win_kv, state_pool, page_table,
           norm_mix_pre, w_in, cmp_pos_k, cmp_w1_k, cmp_w2_k, cmp_pos_v, cmp_w1_v, cmp_w2_v,
           pool_w, pool_scale, w_o, norm_mix_post, norm_mlp_pre, w_up, w_down, norm_mlp_post):
    f = lambda a: np.ascontiguousarray(np.asarray(a, dtype=np.float32))
    x_prompt = f(x_prompt); x_sample = f(x_sample)
    B, T, Dm = x_prompt.shape
    nS = x_sample.shape[0]
    ncore = 8
    _t0 = time.time()
    cache_rows = f(cache_kv)[0].reshape(-1, 1024)
    nrows = cache_rows.shape[0]
    nc = build_nc(nrows)
    ptab = np.ascontiguousarray(np.asarray(page_table, dtype=np.int32))
    tks = np.zeros((16, 2, 32), np.float32)
    tks[:, 0, 1:31] = 1.0
    tks[:, 1, 0] = 1e4
    tks[:, 1, 31] = 1e4
    tnew = np.full((128, 1), NEG, np.float32)
    tnew[0, 0] = 0.0
    st_win = f(state_win_kv)[0].reshape(nS, 512, 512)
    st_pool = f(state_pool)[0]
    cs_s = np.repeat(_rope_table(np.array([2048])), 128, axis=0)
    shared = {
        "g_pre": f(norm_mix_pre)[0:1], "g_post": f(norm_mix_post)[0:1], "g_mpre": f(norm_mlp_pre)[0:1], "g_mpost": f(norm_mlp_post)[0:1],
        "w_in": f(w_in)[0], "w_o": f(w_o)[0], "w_up": f(w_up)[0], "w_down": f(w_down)[0],
        "pool_w": f(pool_w)[0], "pool_sc": f(pool_scale)[0],
        "cw1k": f(cmp_w1_k)[0], "cw1v": f(cmp_w1_v)[0], "cw2k": f(cmp_w2_k)[0], "cw2v": f(cmp_w2_v)[0],
        "cposk": f(cmp_pos_k)[0], "cposv": f(cmp_pos_v)[0], "cs_s": cs_s,
        "cacheC": np.ascontiguousarray(cache_rows[:, 0:512]), "cacheS": np.ascontiguousarray(cache_rows[:, 512:1024]), "t_iota": np.arange(128, dtype=np.float32).reshape(128, 1), "t_new": tnew, "t_tks": tks.reshape(16, 64),
    }
    tabs = [_tables(0), _tables(1)]
    zeros_ctx = np.zeros((1024, Dm), np.float32)
    in_maps = []
    for c in range(ncore):
        b, hf = c // 2, c % 2
        t0 = hf * 1024
        m = dict(shared)
        m.update(tabs[hf])
        m.update({
            "xo": x_prompt[b, t0:t0 + 1024],
            "xc": x_prompt[b, 0:1024] if hf == 1 else zeros_ctx,
            "xs": x_sample[c * NS:(c + 1) * NS, 0],
            "cs_o": _rope_table(np.arange(t0, t0 + 1024)),
            "cs_c": _rope_table(np.arange(t0 - 1024, t0)),
            "st_win": st_win[c * NS:(c + 1) * NS],
            "st_pool": st_pool[c * NS:(c + 1) * NS].reshape(NS * 15, 1024),
            "pt": ptab[c * NS:(c + 1) * NS].reshape(1, 256),
        })
        in_maps.append(m)
    _t1 = time.time()
    res = run_bass_kernel_spmd(nc, in_maps, core_ids=list(range(ncore)))
    print(f"[kernel] build {_t1 - _t0:.1f}s run {time.time() - _t1:.1f}s", flush=True)
    R = res.results
    if DEBUG:
        kernel.debug = R
    y_p = np.zeros((B, T, Dm), np.float32)
    y_s = np.zeros((nS, 1, Dm), np.float32)
    kv_p = np.zeros((1, B, T, 4, 2, 128), np.float32)
    kv_s = np.zeros((1, nS, 1, 4, 2, 128), np.float32)
    win_p = np.zeros((1, B, 512, 2, 2, 128), np.float32)
    win_s = np.zeros((1, nS, 512, 2, 2, 128), np.float32)
    pool_p = np.zeros((1, B, 15, 1024), np.float32)
    pool_s = np.zeros((1, nS, 15, 1024), np.float32)
    for c in range(ncore):
        b, hf = c // 2, c % 2
        t0 = hf * 1024
        y_p[b, t0:t0 + 1024] = R[c]["yp"]
        y_s[c * NS:(c + 1) * NS, 0] = R[c]["ys"]
        kv_p[0, b, t0:t0 + 1024] = R[c]["kvp"].reshape(1024, 4, 2, 128)
        kv_s[0, c * NS:(c + 1) * NS, 0] = R[c]["kvs"].reshape(NS, 4, 2, 128)
        win_s[0, c * NS:(c + 1) * NS] = R[c]["wins"].reshape(NS, 512, 2, 2, 128)
        pool_s[0, c * NS:(c + 1) * NS] = R[c]["pools"]
        if hf == 1:
            win_p[0, b] = R[c]["winp"].reshape(512, 2, 2, 128)
            pool_p[0, b] = R[c]["poolp"]
    return (y_p, y_s, kv_p, kv_s, win_p, win_s, pool_p, pool_s)
```
